# Optimizing a Trainium2 kernel written in Bass

```python
import jax, jax.numpy as jnp
from jax import lax
import numpy as np

D_MODEL = 1024
BATCH = 8
SEQ = 2048
DEPTH = 1

D_PLE = 256
D_FF = 2816
MLSTM_HEADS = 4
MLSTM_HEAD_DIM = 128
MLSTM_WIDTH = MLSTM_HEADS * MLSTM_HEAD_DIM
GMLP_HEADS = 4
GMLP_HEAD_DIM = 128
GMLP_WIDTH = GMLP_HEADS * GMLP_HEAD_DIM
MIX_WIDTH = MLSTM_WIDTH + GMLP_WIDTH
CHUNK = 128
CONV_WIDTH = 4
N_NORMS = 8
EPS = 1e-6
IN_COLS = 4 * MLSTM_WIDTH + 2 * MLSTM_HEADS + 2 * GMLP_WIDTH

kernel_name = "hybrid_mlstm_gmlp_macaron_block"


def rms_norm(x, g):
    xf = x.astype(jnp.float32)
    y = xf * lax.rsqrt(jnp.mean(xf * xf, axis=-1, keepdims=True) + EPS)
    return (y * g.astype(jnp.float32)).astype(x.dtype)


def layer_norm(x, g, b):
    xf = x.astype(jnp.float32)
    mu = jnp.mean(xf, axis=-1, keepdims=True)
    var = jnp.mean(jnp.square(xf - mu), axis=-1, keepdims=True)
    y = (xf - mu) * lax.rsqrt(var + EPS)
    return (y * g.astype(jnp.float32) + b.astype(jnp.float32)).astype(x.dtype)


def swiglu(x, w_gu, w_down):
    g, u = jnp.split(x @ w_gu, 2, axis=-1)
    return (jax.nn.silu(g) * u) @ w_down


def causal_depthwise_conv(x, w, b):
    k_w, c = w.shape
    y = lax.conv_general_dilated(x, w[:, None, :], window_strides=(1,),
                                 padding=[(k_w - 1, 0)],
                                 dimension_numbers=('NWC', 'WIO', 'NWC'),
                                 feature_group_count=c)
    return y + b


def mlstm_chunkwise(q, k, v, log_i, log_f):
    bsz, nh, s, d = q.shape
    nc = s // CHUNK
    q = q.reshape(bsz, nh, nc, CHUNK, d)
    k = k.reshape(bsz, nh, nc, CHUNK, d)
    v = v.reshape(bsz, nh, nc, CHUNK, d)
    log_i = log_i.reshape(bsz, nh, nc, CHUNK)
    log_f = log_f.reshape(bsz, nh, nc, CHUNK)
    b = jnp.cumsum(log_f, axis=-1)
    a = b[..., -1]
    causal = jnp.tril(jnp.ones((CHUNK, CHUNK), dtype=bool))
    d_intra = jnp.where(causal, b[..., :, None] - b[..., None, :] + log_i[..., None, :], -jnp.inf)
    w_state = a[..., None] - b + log_i
    m_loc = jnp.max(w_state, axis=-1)
    e_state = jnp.exp(w_state - m_loc[..., None])
    ke = k * e_state[..., None]
    c_loc = jnp.einsum('bhcld,bhcle->bhcde', ke, v)
    n_loc = jnp.sum(ke, axis=3)

    def step(carry, xs):
        c_prev, n_prev, m_prev = carry
        a_c, m_l, c_l, n_l = xs
        m_new = jnp.maximum(a_c + m_prev, m_l)
        s_prev = jnp.exp(a_c + m_prev - m_new)
        s_loc = jnp.exp(m_l - m_new)
        c_new = s_prev[..., None, None] * c_prev + s_loc[..., None, None] * c_l
        n_new = s_prev[..., None] * n_prev + s_loc[..., None] * n_l
        return (c_new, n_new, m_new), (c_prev, n_prev, m_prev)

    init = (jnp.zeros((bsz, nh, d, d), jnp.float32),
            jnp.zeros((bsz, nh, d), jnp.float32),
            jnp.zeros((bsz, nh), jnp.float32))
    xs = (jnp.moveaxis(a, 2, 0), jnp.moveaxis(m_loc, 2, 0),
          jnp.moveaxis(c_loc, 2, 0), jnp.moveaxis(n_loc, 2, 0))
    _, (c_in, n_in, m_in) = lax.scan(step, init, xs)
    c_in = jnp.moveaxis(c_in, 0, 2)
    n_in = jnp.moveaxis(n_in, 0, 2)
    m_in = jnp.moveaxis(m_in, 0, 2)

    inter_log = b + m_in[..., None]
    m_t = jnp.maximum(inter_log, jnp.max(d_intra, axis=-1))
    e_inter = jnp.exp(inter_log - m_t)
    e_intra = jnp.exp(d_intra - m_t[..., None])
    qk = jnp.einsum('bhctd,bhcsd->bhcts', q, k) * e_intra
    num = (e_inter[..., None] * jnp.einsum('bhctd,bhcde->bhcte', q, c_in)
           + jnp.einsum('bhcts,bhcse->bhcte', qk, v))
    den = e_inter * jnp.einsum('bhctd,bhcd->bhct', q, n_in) + jnp.sum(qk, axis=-1)
    h = num / jnp.maximum(jnp.abs(den), jnp.exp(-m_t))[..., None]
    return h.reshape(bsz, nh, s, d)


def token_mixer(a, w_in, conv_w, conv_b, b_if, mh_norm_g, gmlp_ln_g, gmlp_ln_b,
                w_spatial, b_spatial, w_out):
    bsz, s, _ = a.shape
    mw, mh, md = MLSTM_WIDTH, MLSTM_HEADS, MLSTM_HEAD_DIM
    proj = a @ w_in
    qk_raw, v_m, o_pre, if_pre, u_g, v_g = jnp.split(
        proj, [2 * mw, 3 * mw, 4 * mw, 4 * mw + 2 * mh, 4 * mw + 2 * mh + GMLP_WIDTH], axis=-1)

    qk = jax.nn.silu(causal_depthwise_conv(qk_raw, conv_w, conv_b))
    q, k = jnp.split(qk, 2, axis=-1)

    def heads(t):
        return t.reshape(bsz, s, mh, md).transpose(0, 2, 1, 3).astype(jnp.float32)

    gates = (if_pre + b_if).astype(jnp.float32)
    log_i = gates[..., :mh].transpose(0, 2, 1)
    log_f = jax.nn.log_sigmoid(gates[..., mh:]).transpose(0, 2, 1)
    h = mlstm_chunkwise(heads(q), heads(k) * (md ** -0.5), heads(v_m), log_i, log_f)
    h = h.transpose(0, 2, 1, 3)
    h = h * lax.rsqrt(jnp.mean(h * h, axis=-1, keepdims=True) + EPS)
    h = h * mh_norm_g.reshape(mh, md).astype(jnp.float32)
    h_m = h.reshape(bsz, s, mw).astype(a.dtype) * jax.nn.sigmoid(o_pre)

    u_g = jax.nn.gelu(u_g, approximate=False)
    v_g = layer_norm(jax.nn.gelu(v_g, approximate=False), gmlp_ln_g, gmlp_ln_b)
    nc = s // CHUNK
    vc = v_g.reshape(bsz, nc, CHUNK, GMLP_HEADS, GMLP_HEAD_DIM)
    causal = jnp.tril(jnp.ones((CHUNK, CHUNK), dtype=bool))
    ws = jnp.where(causal, w_spatial, jnp.zeros_like(w_spatial))
    sv = jnp.einsum('gts,bcsge->bctge', ws, vc) + b_spatial.T[:, :, None]
    h_g = u_g * sv.reshape(bsz, s, GMLP_WIDTH)

    return jnp.concatenate([h_m, h_g], axis=-1) @ w_out


def setup_inputs(seed: int = 0) -> dict:
    key = jax.random.key(seed)
    ks = jax.random.split(key, 24)
    f32 = jnp.float32
    nrm = lambda k, shape, scale: (jax.random.normal(k, shape, f32) * scale)
    L = DEPTH
    x = nrm(ks[0], (BATCH, SEQ, D_MODEL), 1.0)
    p = nrm(ks[1], (L, BATCH, SEQ, D_PLE), 1.0)
    ffn1_gu = nrm(ks[2], (L, D_MODEL, 2 * D_FF), D_MODEL ** -0.5)
    ffn1_down = nrm(ks[3], (L, D_FF, D_MODEL), D_FF ** -0.5)
    ffn2_gu = nrm(ks[4], (L, D_MODEL, 2 * D_FF), D_MODEL ** -0.5)
    ffn2_down = nrm(ks[5], (L, D_FF, D_MODEL), D_FF ** -0.5)
    w_in = nrm(ks[6], (L, D_MODEL, IN_COLS), D_MODEL ** -0.5)
    conv_w = nrm(ks[7], (L, CONV_WIDTH, 2 * MLSTM_WIDTH), CONV_WIDTH ** -0.5)
    conv_b = nrm(ks[8], (L, 2 * MLSTM_WIDTH), 0.02)
    b_i = nrm(ks[9], (L, MLSTM_HEADS), 0.1)
    b_f = 3.0 + nrm(ks[10], (L, MLSTM_HEADS), 0.5)
    b_if = jnp.concatenate([b_i, b_f], axis=-1)
    mh_norm_g = 1.0 + nrm(ks[11], (L, MLSTM_WIDTH), 0.02)
    gmlp_ln_g = 1.0 + nrm(ks[12], (L, GMLP_WIDTH), 0.02)
    gmlp_ln_b = nrm(ks[13], (L, GMLP_WIDTH), 0.02)
    w_spatial = nrm(ks[14], (L, GMLP_HEADS, CHUNK, CHUNK), CHUNK ** -0.5)
    b_spatial = 1.0 + nrm(ks[15], (L, GMLP_HEADS, CHUNK), 0.1)
    w_out = nrm(ks[16], (L, MIX_WIDTH, D_MODEL), MIX_WIDTH ** -0.5)
    w_ple = nrm(ks[17], (L, D_PLE, D_MODEL), D_PLE ** -0.5)
    w_ple_gate = nrm(ks[18], (L, D_MODEL, D_MODEL), D_MODEL ** -0.5)
    norm_g = 1.0 + nrm(ks[19], (L, N_NORMS, D_MODEL), 0.02)
    return {"x": x, "p": p, "ffn1_gu": ffn1_gu, "ffn1_down": ffn1_down,
            "ffn2_gu": ffn2_gu, "ffn2_down": ffn2_down, "w_in": w_in,
            "conv_w": conv_w, "conv_b": conv_b, "b_if": b_if,
            "mh_norm_g": mh_norm_g, "gmlp_ln_g": gmlp_ln_g, "gmlp_ln_b": gmlp_ln_b,
            "w_spatial": w_spatial, "b_spatial": b_spatial, "w_out": w_out,
            "w_ple": w_ple, "w_ple_gate": w_ple_gate, "norm_g": norm_g}


def reference(x, p, ffn1_gu, ffn1_down, ffn2_gu, ffn2_down, w_in, conv_w, conv_b,
              b_if, mh_norm_g, gmlp_ln_g, gmlp_ln_b, w_spatial, b_spatial, w_out,
              w_ple, w_ple_gate, norm_g):
    h = x
    for i in range(DEPTH):
        g = norm_g[i]
        h = h + 0.5 * rms_norm(swiglu(rms_norm(h, g[0]), ffn1_gu[i], ffn1_down[i]), g[1])
        mix = token_mixer(rms_norm(h, g[2]), w_in[i], conv_w[i], conv_b[i], b_if[i],
                          mh_norm_g[i], gmlp_ln_g[i], gmlp_ln_b[i], w_spatial[i],
                          b_spatial[i], w_out[i])
        h = h + rms_norm(mix, g[3])
        h = h + 0.5 * rms_norm(swiglu(rms_norm(h, g[4]), ffn2_gu[i], ffn2_down[i]), g[5])
        gate = jax.nn.sigmoid(rms_norm(h, g[6]) @ w_ple_gate[i])
        h = h + rms_norm(gate * (p[i] @ w_ple[i]), g[7])
    return h
```

```python
import numpy as np
from contextlib import ExitStack
import concourse.bass as bass
import concourse.mybir as mybir

F32 = mybir.dt.float32
BF16 = mybir.dt.bfloat16
AF = mybir.ActivationFunctionType
ALU = mybir.AluOpType
AX = mybir.AxisListType


ENGS = ("pe", "act", "dve", "pool", "sp")


class Sched:
    def __init__(self, nc, es):
        self.nc = nc
        self.es = es
        self.sem = {k: es.enter_context(nc.semaphore("s_" + k)) for k in ENGS}
        self.cnt = {k: 0 for k in ENGS}
        self.seen = {k: {} for k in ENGS}
        self.res = {}
        self.prog = {k: [] for k in ENGS}
        self.dsem = {}
        self.pending_inc = {k: None for k in ENGS}
        self.nwaits = 0
        self.excl = set()

    def _r(self, key):
        r = self.res.get(key)
        if r is None:
            r = {"w": None, "r": []}
            self.res[key] = r
        return r

    def _deps(self, eng, reads, writes):
        deps = {}

        def add(d):
            if d is None:
                return
            k, v = d
            if deps.get(k, 0) < v:
                deps[k] = v

        for key in reads:
            r = self._r(key)
            add(r["w"])
            if key in self.excl:
                for d in r["r"]:
                    if d[0] != eng:
                        add(d)
        for key in writes:
            r = self._r(key)
            add(r["w"])
            for d in r["r"]:
                add(d)
        out = []
        for k, v in deps.items():
            if eng == "pe" and k == "pe":
                continue
            if self.seen[eng].get(k, 0) >= v:
                continue
            self.seen[eng][k] = v
            out.append((k, v))
        return out

    def _semobj(self, k):
        if k in self.sem:
            return self.sem[k]
        return self.dsem[k][0]

    def op(self, eng, fn, reads=(), writes=(), inc=True):
        waits = self._deps(eng, reads, writes)
        self.nwaits += len(waits)
        if inc:
            self.cnt[eng] += 1
            stamp = (eng, self.cnt[eng])
        else:
            stamp = (eng, self.cnt[eng] + 1)
        for key in reads:
            self._r(key)["r"].append(stamp)
        for key in writes:
            r = self._r(key)
            r["w"] = stamp
            r["r"] = []
        sem = self.sem[eng]
        wl = [(self._semobj(k), v) for k, v in waits]

        def emit(e):
            for s, v in wl:
                e.wait_ge(s, v)
            ins = fn(e)
            if inc:
                ins.then_inc(sem, 1)

        self.prog[eng].append(emit)
        if not inc:
            self.pending_inc[eng] = True
        else:
            self.pending_inc[eng] = None

    def dma(self, eng, fn, reads=(), writes=(), n=1):
        waits = self._deps(eng, reads, writes)
        self.nwaits += len(waits)
        dkey = "d_" + str(writes[0])
        if dkey not in self.dsem:
            nm = "dm%d" % len(self.dsem)
            self.dsem[dkey] = [self.es.enter_context(self.nc.semaphore(nm)), 0]
        ent = self.dsem[dkey]
        ent[1] += 16 * n
        stamp = (dkey, ent[1])
        for key in reads:
            self._r(key)["r"].append(stamp)
        for key in writes:
            r = self._r(key)
            r["w"] = stamp
            r["r"] = []
        dsem = ent[0]
        wl = [(self._semobj(k), v) for k, v in waits]

        def emit(e):
            for s, v in wl:
                e.wait_ge(s, v)
            ins = fn(e)
            if not isinstance(ins, (list, tuple)):
                ins = [ins]
            assert len(ins) == n, (len(ins), n)
            for i in ins:
                i.then_inc(dsem, 16)

        self.prog[eng].append(emit)

    def final_wait(self, eng, keys):
        waits = self._deps(eng, keys, ())
        wl = [(self._semobj(k), v) for k, v in waits]

        def emit(e):
            for s, v in wl:
                e.wait_ge(s, v)

        self.prog[eng].append(emit)

    def run(self):
        nc = self.nc
        with nc.Block() as block:
            @block.tensor
            def _(e):
                for f in self.prog["pe"]:
                    f(e)

            @block.scalar
            def _(e):
                for f in self.prog["act"]:
                    f(e)

            @block.vector
            def _(e):
                for f in self.prog["dve"]:
                    f(e)

            @block.gpsimd
            def _(e):
                for f in self.prog["pool"]:
                    f(e)

            @block.sync
            def _(e):
                for f in self.prog["sp"]:
                    f(e)


T = 2048
TH = 1024
NTT = 8
D = 1024
DFF = 2816
NFC = 22
EPS = 1e-6
NRING = 3
QSCALE = 128 ** -0.5


def build(stage=4, nhalf=2, mix_level=4):
    nc = bass.Bass("TRN2", target_bir_lowering=False)
    dt_in = lambda name, shape: nc.dram_tensor(name, shape, F32, kind="ExternalInput").ap()
    x = dt_in("x", [T, D])
    p = dt_in("p", [T, 256])
    ffn_gu = {1: dt_in("ffn1_gu", [D, 2 * DFF]), 2: dt_in("ffn2_gu", [D, 2 * DFF])}
    ffn_dn = {1: dt_in("ffn1_down", [DFF, D]), 2: dt_in("ffn2_down", [DFF, D])}
    w_in = dt_in("w_in", [D, 3080])
    conv_w = dt_in("conv_w", [4, 1024])
    conv_b = dt_in("conv_b", [1024])
    b_if = dt_in("b_if", [8])
    mh_g = dt_in("mh_norm_g", [512])
    ln_g = dt_in("gmlp_ln_g", [512])
    ln_b = dt_in("gmlp_ln_b", [512])
    w_sp = dt_in("w_spatial", [4, 128, 128])
    b_sp = dt_in("b_spatial", [4, 128])
    w_out = dt_in("w_out", [D, D])
    w_ple = dt_in("w_ple", [256, D])
    w_pg = dt_in("w_ple_gate", [D, D])
    norm_g = dt_in("norm_g", [8, D])
    ident = dt_in("ident", [128, 128])
    utm = dt_in("utm", [128, 128])
    y = nc.dram_tensor("y", [T, D], F32, kind="ExternalOutput").ap()

    with ExitStack() as es:
        S = Sched(nc, es)
        sb = lambda name, shape, dt: es.enter_context(nc.sbuf_tensor(name, shape, dt))
        ps = lambda name, shape, dt: es.enter_context(nc.psum_tensor(name, shape, dt))

        def V(fn, r=(), w=()):
            S.op("dve", fn, reads=r, writes=w)

        def A(fn, r=(), w=()):
            S.op("act", fn, reads=r, writes=w)

        def G(fn, r=(), w=()):
            S.op("dve", fn, reads=r, writes=w)

        def PE(fn, r=(), w=(), inc=True):
            S.op("pe", fn, reads=r, writes=w, inc=inc)

        h = sb("h", [128, NTT, D], F32)
        xnT = sb("xnT", [128, 8, TH], BF16)
        R = sb("R", [128, 22528], BF16)
        WD = sb("WD", [128, 22528], BF16)
        ring = [sb("ring%d" % i, [128, 8, 256], BF16) for i in range(NRING)]
        wple = sb("wple", [128, 2, D], BF16)
        pTs = sb("pTs", [128, 2, TH], BF16)
        gpost = [sb("gpost%d" % i, [128, D], F32) for i in range(2)]
        gcol = sb("gcol", [128, 4, 8], F32)
        xnb = [sb("xnb%d" % i, [128, D], BF16) for i in range(2)]
        scr = [sb("scr%d" % i, [128, D], F32) for i in range(2)]
        sg = [sb("sg%d" % i, [128, 512], F32) for i in range(2)]
        stt = sb("stt", [128, 16, 8], F32)
        ident_f = sb("ident_f", [128, 128], F32)
        ident_b = sb("ident_b", [128, 128], BF16)
        ut_f = sb("ut_f", [128, 128], F32)
        ones_f = sb("ones_f", [128, 128], F32)
        maskS = sb("maskS", [128, 128], F32)
        cw = sb("cw", [128, 4, 8], F32)
        cb = sb("cb", [128, 8], F32)
        bif = sb("bif", [128, 8], F32)
        mhg = sb("mhg", [128, 512], F32)
        lng = sb("lng", [128, 512], F32)
        lnb = sb("lnb", [128, 512], F32)
        wsT = sb("wsT", [128, 4, 128], BF16)
        bsp = sb("bsp", [128, 4, 128], F32)
        wif = sb("wif", [128, 8, 8], BF16)
        Cst = sb("Cst", [128, 4, 130], F32)
        Cbf = sb("Cbf", [128, 4, 130], BF16)
        carry = sb("carry", [128, 8, 4], F32)
        pt = [sb("pt%d" % i, [128, 256], F32) for i in range(2)]
        ptb = [sb("ptb%d" % i, [128, 256], BF16) for i in range(2)]
        fz = sb("fz", [128, 4], F32)
        gnames = ["gat8", "li", "fpv", "gab", "ge1", "gl1", "gmn", "lf", "bcs", "abc", "gg", "Gb", "mloc", "amp",
                  "d1", "sprev", "d2", "sloc", "Mc", "d3", "cscale", "d4", "ehat", "d5", "estate", "d6", "clampv"]
        GA = {}
        for nm in gnames:
            GA[nm] = sb("g_" + nm, [128, 8, 8 if nm == "gat8" else 4], F32)
        m_in = sb("m_in", [128, 9, 4], F32)
        gmax = sb("gmax", [128, 1], F32)
        dg = sb("dg", [128, 32], F32)
        ggpad = sb("ggpad", [128, 128], F32)
        mst = sb("mst", [128, 8, 16], F32)

        hT = R[:, :].rearrange("p (f t) -> p f t", f=NFC)
        qkT = R[:, 0:8192].rearrange("p (j t) -> p j t", j=8)
        og = R[:, 8192:12288].rearrange("p (c e) -> p c e", c=8)
        uT = R[:, 12288:16384].rearrange("p (j t) -> p j t", j=4)
        vg = R[:, 16384:20480].rearrange("p (c e) -> p c e", c=8)
        gate = R[:, 0:16384].bitcast(F32).rearrange("p (c e) -> p c e", c=8)
        Wdn = WD[:, :].rearrange("p (f n) -> p f n", f=NFC)
        wout = WD[:, 0:8192].rearrange("p (m n) -> p m n", m=8)
        vext = WD[:, 8192:12352].rearrange("p (c h e) -> p c h e", c=8, h=4)
        stg = [WD[:, 12352 + i * 2056: 12352 + (i + 1) * 2056].bitcast(F32) for i in range(2)]
        o0 = 16464
        ke = [WD[:, o0 + i * 512: o0 + (i + 1) * 512].rearrange("p (h d) -> p h d", h=4) for i in range(2)]
        o0 += 1024
        vp = [WD[:, o0 + i * 520: o0 + (i + 1) * 520].rearrange("p (h e) -> p h e", h=4) for i in range(2)]
        o0 += 1040
        qks = [WD[:, o0 + i * 512: o0 + (i + 1) * 512].rearrange("p (h d) -> p h d", h=4) for i in range(2)]
        o0 += 1024
        hmb = [WD[:, o0 + i * 512: o0 + (i + 1) * 512].rearrange("p (h d) -> p h d", h=4) for i in range(2)]
        o0 += 1024
        tmpC = WD[:, o0: o0 + 1040].bitcast(F32).rearrange("p (h e) -> p h e", h=4)
        o0 += 1040
        assert o0 <= 22528

        Pacc = [ps("P01", [128, 1024], F32), ps("P23", [128, 1024], F32)]
        PaccK = [["P01a", "P01b"], ["P23a", "P23b"]]
        Tb = [ps("T0", [128, 512], F32), ps("T1", [128, 512], F32)]
        TbK = ["T0", "T1"]
        Mb = [ps("M0", [128, 512], F32), ps("M1", [128, 512], F32)]
        MbK = ["M0", "M1"]
        S.excl = {"P01a", "P01b", "P23a", "P23b", "T0", "T1", "M0", "M1"}
        Tbv = [t[:, :].bitcast(BF16).rearrange("p (k n) -> p k n", k=8) for t in Tb]

        st_i = [0]

        def newstat():
            st_i[0] = (st_i[0] + 1) % 16
            return stt[:, st_i[0], :], ("st", st_i[0])

        def ld(eng, out_ap, in_ap, key, slow=False):
            S.dma(eng, lambda e: e.dma_start(out=out_ap, in_=in_ap, allow_slow_non_contiguous=slow), writes=[key])

        ld("sp", ident_f[:], ident, "ident_f")
        ld("sp", ut_f[:], utm, "ut_f")
        for a_ in range(4):
            ld("sp", gcol[:, a_, :], norm_g[2 * a_].rearrange("(k p) -> p k", p=128), "gcol", slow=True)
        for k_ in range(4):
            ld("sp", cw[:, k_, :], conv_w[k_].rearrange("(j p) -> p j", p=128), "cw", slow=True)
        ld("sp", cb[:], conv_b.rearrange("(j p) -> p j", p=128), "cb", slow=True)
        ld("sp", bif[:], b_if.partition_broadcast(128), "bif")
        ld("sp", mhg[:], mh_g.partition_broadcast(128), "mhg")
        ld("sp", lng[:], ln_g.partition_broadcast(128), "lng")
        ld("sp", lnb[:], ln_b.partition_broadcast(128), "lnb")
        ld("sp", bsp[:], b_sp.rearrange("g t -> (g t)").partition_broadcast(128), "bsp")
        ld("sp", scr[0][:, 0:512].rearrange("p (g s) -> p g s", g=4), w_sp.rearrange("g t s -> t g s"), ("scr", 0))
        S.dma("pool", lambda e: e.dma_start(out=wif[:], in_=w_in[:, 2048:2056].rearrange("(k p) n -> p k n", p=128)),
              writes=["wif"])
        S.dma("pool", lambda e: e.dma_start(out=wple[:], in_=w_ple.rearrange("(k p) n -> p k n", p=128)),
              writes=["wple"])

        V(lambda e: e.tensor_copy(out=ident_b[:], in_=ident_f[:]), r=["ident_f"], w=["ident_b"])
        V(lambda e: e.memset(ones_f[:], 1.0), w=["ones_f"])
        V(lambda e: e.tensor_scalar(out=maskS[:], in0=ut_f[:], scalar1=QSCALE, scalar2=None, op0=ALU.mult),
          r=["ut_f"], w=["maskS"])
        V(lambda e: e.memset(Cst[:], 0.0), w=["Cst"])
        V(lambda e: e.memset(carry[:], 0.0), w=["carry"])
        V(lambda e: e.memset(m_in[:], 0.0), w=["m_in"])
        V(lambda e: e.memset(fz[:], 0.0), w=["fz"])
        V(lambda e: e.memset(ggpad[:], 0.0), w=["ggpad"])
        V(lambda e: e.memset(dg[:], 0.0), w=["dg"])
        M0v4 = Mb[0][:, :].rearrange("p (h n) -> p h n", h=4)
        for g_ in range(4):
            PE(lambda e, g_=g_: e.transpose(out=M0v4[:, g_, :], in_=scr[0][:, g_ * 128:(g_ + 1) * 128], identity=ident_f[:]),
               r=[("scr", 0), "ident_f"], w=["M0"], inc=(g_ == 3))
        V(lambda e: e.tensor_tensor(out=wsT[:], in0=M0v4, in1=ut_f[:].unsqueeze(1).to_broadcast([128, 4, 128]), op=ALU.mult),
          r=["M0", "ut_f"], w=["wsT"])

        ring_specs = []

        def ring_add(srcs):
            ring_specs.append(srcs)
            return len(ring_specs) - 1

        ring_issued = [0]

        def ring_need(i):
            while ring_issued[0] < len(ring_specs) and ring_issued[0] <= i + NRING - 1:
                j = ring_issued[0]
                slot = j % NRING
                srcs = ring_specs[j]

                def fn(e, srcs=srcs, slot=slot):
                    out = []
                    for (c0, c1, src) in srcs:
                        out.append(e.dma_start(out=ring[slot][:, :, c0:c1], in_=src.rearrange("(k p) n -> p k n", p=128)))
                    return out

                S.dma("pool", fn, writes=[("ring", slot)], n=len(srcs))
                ring_issued[0] += 1
            return i % NRING

        seq = {}
        for hf in range(nhalf):
            for f in (1, 2):
                if f == 1 or stage >= 3:
                    seq[(hf, "gu", f)] = [ring_add([(0, 128, ffn_gu[f][:, fc * 128:(fc + 1) * 128]),
                                                     (128, 256, ffn_gu[f][:, DFF + fc * 128: DFF + (fc + 1) * 128])])
                                          for fc in range(NFC)]
                if f == 1 and stage >= 2:
                    cols = [1024, 1280, 1536, 1792, 2568, 2824, 0, 256, 512, 768, 2056, 2312]
                    seq[(hf, "win")] = [ring_add([(0, 256, w_in[:, c:c + 256])]) for c in cols]
            if stage >= 4:
                seq[(hf, "pg")] = [ring_add([(0, 256, w_pg[:, c * 256:(c + 1) * 256])]) for c in range(4)]

        fence_n = [0]

        def fence(key):
            fence_n[0] += 1
            V(lambda e: e.memset(fz[:, 0:1], 0.0), w=[key, "fzs"])

        def load_gpost(slot, idx, scale):
            S.dma("sp", lambda e: e.dma_start(out=gpost[slot][:], in_=norm_g[idx].partition_broadcast(128)),
                  writes=[("gpost", slot)])
            if scale != 1.0:
                V(lambda e: e.tensor_scalar(out=gpost[slot][:], in0=gpost[slot][:], scalar1=scale, scalar2=None, op0=ALU.mult),
                  r=[("gpost", slot)], w=[("gpost", slot)])

        def rstd_from_ssq(s, sk, n, col_in=0):
            V(lambda e: e.tensor_scalar(out=s[:, 1:2], in0=s[:, col_in:col_in + 1], scalar1=1.0 / n, scalar2=EPS,
                                        op0=ALU.mult, op1=ALU.add), r=[sk], w=[sk])
            A(lambda e: e.activation(out=s[:, 2:3], in_=s[:, 1:2], func=AF.Sqrt), r=[sk], w=[sk])
            V(lambda e: e.reciprocal(out=s[:, 3:4], in_=s[:, 2:3]), r=[sk], w=[sk])

        def prenorm(tt, gi):
            s, sk = newstat()
            sl = tt % 2
            A(lambda e: e.activation(out=scr[sl][:], in_=h[:, tt, :], func=AF.Square, accum_out=s[:, 0:1]),
              r=[("h", tt)], w=[("scr", sl), sk])
            rstd_from_ssq(s, sk, D)
            V(lambda e: e.tensor_scalar(out=xnb[sl][:], in0=h[:, tt, :], scalar1=s[:, 3:4], scalar2=None, op0=ALU.mult),
              r=[("h", tt), sk], w=[("xnb", sl)])
            for k in range(8):
                PE(lambda e, k=k: e.transpose(out=Tbv[sl][:, k, :], in_=xnb[sl][:, k * 128:(k + 1) * 128], identity=ident_b[:]),
                   r=[("xnb", sl), "ident_b"], w=[TbK[sl]], inc=(k == 7))
            V(lambda e: e.tensor_tensor(out=xnT[:, :, tt * 128:(tt + 1) * 128], in0=Tbv[sl],
                                        in1=gcol[:, gi, :].unsqueeze(2).to_broadcast([128, 8, 128]), op=ALU.mult),
              r=[TbK[sl], "gcol"], w=[("xnT", tt)])

        def epilogue(P, PK, tt, gslot):
            s, sk = newstat()
            sl = tt % 2
            A(lambda e: e.activation(out=xnb[sl][:], in_=P[:, :], func=AF.Square, accum_out=s[:, 0:1]),
              r=PK, w=[("xnb", sl), sk])
            rstd_from_ssq(s, sk, D)
            V(lambda e: e.scalar_tensor_tensor(out=scr[sl][:], in0=P[:, :], scalar=s[:, 3:4], in1=gpost[gslot][:],
                                               op0=ALU.mult, op1=ALU.mult),
              r=PK + [sk, ("gpost", gslot)], w=[("scr", sl)])
            V(lambda e: e.tensor_tensor(out=h[:, tt, :], in0=h[:, tt, :], in1=scr[sl][:], op=ALU.add),
              r=[("h", tt), ("scr", sl)], w=[("h", tt)])

        def ffn(hf, f, gi, gslot, gidx):
            load_gpost(gslot, gidx, 0.5)
            for tt in range(NTT):
                prenorm(tt, gi)
            fence("Rg")
            fence("Wg")
            tiles = seq[(hf, "gu", f)]
            for fc in range(NFC):
                slot = ring_need(tiles[fc])
                S.dma("pool", lambda e, fc=fc: e.dma_start(out=Wdn[:, fc, :], in_=ffn_dn[f][fc * 128:(fc + 1) * 128, :]),
                      reads=["Wg"], writes=[("Wdn", fc)])
                for tb in range(2):
                    j = (fc * 2 + tb) % 2
                    P, PK = Pacc[j], PaccK[j]
                    xk = [("xnT", tb * 4 + i) for i in range(4)]
                    for k in range(8):
                        PE(lambda e, k=k, P=P, slot=slot, tb=tb: e.matmul(
                            P[:, 0:512], lhsT=ring[slot][:, k, 0:128], rhs=xnT[:, k, tb * 512:(tb + 1) * 512],
                            start=(k == 0), stop=(k == 7)), r=[("ring", slot)] + xk, w=[PK[0]], inc=False)
                    for k in range(8):
                        PE(lambda e, k=k, P=P, slot=slot, tb=tb: e.matmul(
                            P[:, 512:1024], lhsT=ring[slot][:, k, 128:256], rhs=xnT[:, k, tb * 512:(tb + 1) * 512],
                            start=(k == 0), stop=(k == 7)), r=[("ring", slot)] + xk, w=[PK[1]], inc=(k == 7))
                    A(lambda e, P=P, j=j: e.activation(out=sg[j][:], in_=P[:, 0:512], func=AF.Silu),
                      r=[PK[0]], w=[("sg", j)])
                    V(lambda e, P=P, j=j, fc=fc, tb=tb: e.tensor_tensor(
                        out=hT[:, fc, tb * 512:(tb + 1) * 512], in0=sg[j][:], in1=P[:, 512:1024], op=ALU.mult),
                      r=[("sg", j), PK[1], "Rg"], w=[("hT", tb)])
            for tt in range(NTT):
                j = tt % 2
                P, PK = Pacc[j], PaccK[j]
                for fc in range(NFC):
                    PE(lambda e, P=P, fc=fc, tt=tt: e.matmul(
                        P[:, 0:512], lhsT=hT[:, fc, tt * 128:(tt + 1) * 128], rhs=Wdn[:, fc, 0:512],
                        start=(fc == 0), stop=(fc == NFC - 1)),
                       r=[("hT", tt // 4), ("Wdn", fc), "Rg", "Wg"], w=[PK[0]], inc=False)
                    PE(lambda e, P=P, fc=fc, tt=tt: e.matmul(
                        P[:, 512:1024], lhsT=hT[:, fc, tt * 128:(tt + 1) * 128], rhs=Wdn[:, fc, 512:1024],
                        start=(fc == 0), stop=(fc == NFC - 1)),
                       r=[("hT", tt // 4), ("Wdn", fc), "Rg", "Wg"], w=[PK[1]], inc=(fc == NFC - 1))
                epilogue(P, PK, tt, gslot)

        def bc3(ap2, n):
            return ap2.unsqueeze(2).to_broadcast([128, ap2.shape[1], n])

        def mixer(hf):
            load_gpost(1, 3, 1.0)
            for tt in range(NTT):
                prenorm(tt, 1)
            fence("Rg")
            fence("Wg")
            S.dma("pool", lambda e: e.dma_start(out=wout, in_=w_out.rearrange("(k p) n -> p k n", p=128)),
                  reads=["Wg"], writes=["wout"])
            V(lambda e: e.memset(vext[:, :, :, 128:130], 1.0), r=["Wg"], w=["vext1"])
            tiles = seq[(hf, "win")]
            RW = ["Rg", "Wg"]

            def tokproj(t0, evac):
                s0 = ring_need(tiles[t0])
                s1 = tiles[t0 + 1] % NRING
                for tt in range(NTT):
                    j = tt % 2
                    P, PK = Pacc[j], PaccK[j]
                    for half, sl_ in ((0, s0), (1, s1)):
                        for k in range(8):
                            PE(lambda e, k=k, P=P, sl_=sl_, half=half, tt=tt: e.matmul(
                                P[:, half * 256:(half + 1) * 256], lhsT=xnT[:, k, tt * 128:(tt + 1) * 128],
                                rhs=ring[sl_][:, k, :], start=(k == 0), stop=(k == 7)),
                               r=[("ring", sl_), ("xnT", tt)], w=[PK[0]], inc=(k == 7 and half == 1))
                    evac(tt, P, PK)

            def evac_v(tt, P, PK):
                A(lambda e: e.activation(out=vext[:, tt, :, 0:128], in_=P[:, 0:512].rearrange("p (h e) -> p h e", h=4),
                                         func=AF.Copy), r=[PK[0]] + RW, w=[("vext", tt)])
                for k in range(8):
                    PE(lambda e, k=k: e.matmul(Mb[0][:, 0:8], lhsT=xnT[:, k, tt * 128:(tt + 1) * 128], rhs=wif[:, k, :],
                                               start=(k == 0), stop=(k == 7)),
                       r=["wif", ("xnT", tt)], w=["M0"], inc=(k == 7))
                V(lambda e: e.tensor_copy(out=GA["gat8"][:, tt, :], in_=Mb[0][:, 0:8]), r=["M0"], w=["gat8"])

            def evac_o(tt, P, PK):
                sl = tt % 2
                A(lambda e: e.activation(out=sg[sl][:], in_=P[:, 0:512], func=AF.Sigmoid), r=[PK[0]], w=[("sg", sl)])
                G(lambda e: e.tensor_tensor(out=og[:, tt, :], in0=sg[sl][:], in1=mhg[:], op=ALU.mult),
                  r=[("sg", sl), "mhg"] + RW, w=[("og", tt)])

            def evac_vg(tt, P, PK):
                sl = tt % 2
                s, sk = newstat()
                A(lambda e: e.activation(out=sg[sl][:], in_=P[:, 0:512], func=AF.Gelu), r=[PK[0]], w=[("sg", sl)])
                V(lambda e: e.bn_stats(out=s[:, 0:6], in_=sg[sl][:]), r=[("sg", sl)], w=[sk])
                s2, sk2 = newstat()
                V(lambda e: e.bn_aggr(out=s2[:, 4:6], in_=s[:, 0:6]), r=[sk], w=[sk2])
                V(lambda e: e.tensor_scalar(out=s2[:, 1:2], in0=s2[:, 5:6], scalar1=EPS, scalar2=None, op0=ALU.add),
                  r=[sk2], w=[sk2])
                A(lambda e: e.activation(out=s2[:, 2:3], in_=s2[:, 1:2], func=AF.Sqrt), r=[sk2], w=[sk2])
                V(lambda e: e.reciprocal(out=s2[:, 3:4], in_=s2[:, 2:3]), r=[sk2], w=[sk2])
                V(lambda e: e.tensor_scalar(out=scr[sl][:, 0:512], in0=sg[sl][:], scalar1=s2[:, 4:5], scalar2=s2[:, 3:4],
                                            op0=ALU.subtract, op1=ALU.mult), r=[("sg", sl), sk2], w=[("scr", sl)])
                G(lambda e: e.tensor_tensor(out=scr[sl][:, 0:512], in0=scr[sl][:, 0:512], in1=lng[:], op=ALU.mult),
                  r=[("scr", sl), "lng"], w=[("scr", sl)])
                G(lambda e: e.tensor_tensor(out=vg[:, tt, :], in0=scr[sl][:, 0:512], in1=lnb[:], op=ALU.add),
                  r=[("scr", sl), "lnb"] + RW, w=[("vg", tt)])

            tokproj(0, evac_v)
            tokproj(2, evac_o)
            tokproj(4, evac_vg)

            def featproj(ti, evac):
                slot = ring_need(tiles[ti])
                for cc in range(2):
                    for tb in range(2):
                        j = (cc * 2 + tb) % 2
                        P, PK = Mb[j], MbK[j]
                        xk = [("xnT", tb * 4 + i) for i in range(4)]
                        for k in range(8):
                            PE(lambda e, k=k, P=P, cc=cc, tb=tb: e.matmul(
                                P[:, :], lhsT=ring[slot][:, k, cc * 128:(cc + 1) * 128],
                                rhs=xnT[:, k, tb * 512:(tb + 1) * 512], start=(k == 0), stop=(k == 7)),
                               r=[("ring", slot)] + xk, w=[PK], inc=(k == 7))
                        evac(cc, tb, P, PK)

            def qk_evac(jbase):
                def ev(cc, tb, P, PK):
                    jj = jbase + cc
                    sl = jj % 2
                    if tb == 0:
                        V(lambda e: e.tensor_copy(out=stg[sl][:, 0:3], in_=carry[:, jj, 0:3]),
                          r=["carry"] + RW, w=[("stg", sl)])
                    A(lambda e: e.activation(out=stg[sl][:, 3 + tb * 512: 3 + (tb + 1) * 512], in_=P[:, :], func=AF.Copy),
                      r=[PK] + RW, w=[("stg", sl)])
                    if tb == 1:
                        acc = scr[sl]
                        V(lambda e: e.tensor_scalar(out=acc[:], in0=stg[sl][:, 3:1027], scalar1=cw[:, 3, jj:jj + 1],
                                                    scalar2=cb[:, jj:jj + 1], op0=ALU.mult, op1=ALU.add),
                          r=[("stg", sl), "cw", "cb"] + RW, w=[("scr", sl)])
                        for kk in (2, 1, 0):
                            V(lambda e, kk=kk: e.scalar_tensor_tensor(out=acc[:], in0=stg[sl][:, kk:kk + 1024],
                                                                      scalar=cw[:, kk, jj:jj + 1], in1=acc[:],
                                                                      op0=ALU.mult, op1=ALU.add),
                              r=[("stg", sl), "cw", ("scr", sl)] + RW, w=[("scr", sl)])
                        A(lambda e: e.activation(out=qkT[:, jj, :], in_=acc[:], func=AF.Silu),
                          r=[("scr", sl)] + RW, w=[("qkT", jj)])
                        V(lambda e: e.tensor_copy(out=carry[:, jj, 0:3], in_=stg[sl][:, 1024:1027]),
                          r=[("stg", sl)] + RW, w=["carry"])
                return ev

            def u_evac(jbase):
                def ev(cc, tb, P, PK):
                    jj = jbase + cc
                    A(lambda e: e.activation(out=uT[:, jj, tb * 512:(tb + 1) * 512], in_=P[:, :], func=AF.Gelu),
                      r=[PK] + RW, w=[("uT", jj)])
                return ev

            featproj(6, qk_evac(0))
            featproj(7, qk_evac(2))
            featproj(8, qk_evac(4))
            featproj(9, qk_evac(6))
            featproj(10, u_evac(0))
            featproj(11, u_evac(2))

            if mix_level < 2:
                return
            g = GA
            fl = lambda nm: g[nm][:, :, :].rearrange("p c h -> p (c h)")
            V(lambda e: e.tensor_tensor(out=g["li"][:], in0=g["gat8"][:, :, 0:4],
                                        in1=bif[:, 0:4].unsqueeze(1).to_broadcast([128, 8, 4]), op=ALU.add),
              r=["gat8", "bif"], w=["li"])
            V(lambda e: e.tensor_tensor(out=g["fpv"][:], in0=g["gat8"][:, :, 4:8],
                                        in1=bif[:, 4:8].unsqueeze(1).to_broadcast([128, 8, 4]), op=ALU.add),
              r=["gat8", "bif"], w=["fpv"])
            V(lambda e: e.scalar_tensor_tensor(out=g["gab"][:], in0=g["fpv"][:], scalar=-1.0, in1=g["fpv"][:],
                                               op0=ALU.mult, op1=ALU.min), r=["fpv"], w=["gab"])
            A(lambda e: e.activation(out=g["ge1"][:], in_=g["gab"][:], func=AF.Exp), r=["gab"], w=["ge1"])
            A(lambda e: e.activation(out=g["gl1"][:], in_=g["ge1"][:], func=AF.Ln, bias=1.0), r=["ge1"], w=["gl1"])
            V(lambda e: e.tensor_single_scalar(out=g["gmn"][:], in_=g["fpv"][:], scalar=0.0, op=ALU.min),
              r=["fpv"], w=["gmn"])
            V(lambda e: e.tensor_tensor(out=g["lf"][:], in0=g["gmn"][:], in1=g["gl1"][:], op=ALU.subtract),
              r=["gmn", "gl1"], w=["lf"])
            PE(lambda e: e.matmul(Mb[0][:, 0:32], lhsT=ut_f[:], rhs=fl("lf"), start=True, stop=True),
               r=["ut_f", "lf"], w=["M0"])
            PE(lambda e: e.matmul(Mb[0][:, 32:64], lhsT=ones_f[:], rhs=fl("lf"), start=True, stop=True),
               r=["ones_f", "lf"], w=["M0"])
            V(lambda e: e.tensor_copy(out=fl("bcs"), in_=Mb[0][:, 0:32]), r=["M0"], w=["bcs"])
            V(lambda e: e.tensor_copy(out=fl("abc"), in_=Mb[0][:, 32:64]), r=["M0"], w=["abc"])
            V(lambda e: e.tensor_tensor(out=g["gg"][:], in0=g["li"][:], in1=g["bcs"][:], op=ALU.subtract),
              r=["li", "bcs"], w=["gg"])
            V(lambda e: e.tensor_copy(out=ggpad[:, 0:32], in_=fl("gg")), r=["gg"], w=["ggpad"])
            PE(lambda e: e.transpose(out=Mb[1][:, 0:128], in_=ggpad[:], identity=ident_f[:]),
               r=["ggpad", "ident_f"], w=["M1"])
            V(lambda e: e.tensor_reduce(out=gmax[0:32, 0:1], in_=Mb[1][0:32, 0:128], axis=AX.X, op=ALU.max),
              r=["M1"], w=["gmax"])
            V(lambda e: e.tensor_scalar(out=dg[0:32, 0:32], in0=ident_f[0:32, 0:32], scalar1=gmax[0:32, 0:1],
                                        scalar2=None, op0=ALU.mult), r=["gmax", "ident_f"], w=["dg"])
            PE(lambda e: e.matmul(Mb[1][:, 128:160], lhsT=ones_f[:], rhs=dg[:, 0:32], start=True, stop=True),
               r=["dg", "ones_f"], w=["M1"])
            V(lambda e: e.tensor_copy(out=fl("Gb"), in_=Mb[1][:, 128:160]), r=["M1"], w=["Gb"])
            V(lambda e: e.tensor_tensor(out=g["mloc"][:], in0=g["abc"][:], in1=g["Gb"][:], op=ALU.add),
              r=["abc", "Gb"], w=["mloc"])
            for c in range(8):
                V(lambda e, c=c: e.tensor_tensor(out=g["amp"][:, c, :], in0=g["abc"][:, c, :], in1=m_in[:, c, :], op=ALU.add),
                  r=["abc", "m_in"], w=["amp"])
                V(lambda e, c=c: e.tensor_tensor(out=m_in[:, c + 1, :], in0=g["amp"][:, c, :], in1=g["mloc"][:, c, :],
                                                 op=ALU.max), r=["amp", "mloc"], w=["m_in"])

            def sub_exp(dn, en, a_ap, b_ap, rk, scale=1.0, op=ALU.subtract):
                V(lambda e: e.tensor_tensor(out=g[dn][:], in0=a_ap(), in1=b_ap(), op=op), r=rk, w=[dn])
                A(lambda e: e.activation(out=g[en][:], in_=g[dn][:], func=AF.Exp, scale=scale), r=[dn], w=[en])

            sub_exp("d1", "sprev", lambda: g["amp"][:], lambda: m_in[:, 1:9, :], ["amp", "m_in"])
            sub_exp("d2", "sloc", lambda: g["mloc"][:], lambda: m_in[:, 1:9, :], ["mloc", "m_in"])
            V(lambda e: e.tensor_tensor(out=g["Mc"][:], in0=m_in[:, 0:8, :], in1=g["Gb"][:], op=ALU.max),
              r=["m_in", "Gb"], w=["Mc"])
            sub_exp("d3", "cscale", lambda: m_in[:, 0:8, :], lambda: g["Mc"][:], ["m_in", "Mc"])
            V(lambda e: e.tensor_scalar(out=g["cscale"][:], in0=g["cscale"][:], scalar1=QSCALE, scalar2=None, op0=ALU.mult),
              r=["cscale"], w=["cscale"])
            sub_exp("d4", "ehat", lambda: g["gg"][:], lambda: g["Mc"][:], ["gg", "Mc"])
            sub_exp("d5", "estate", lambda: g["gg"][:], lambda: g["Gb"][:], ["gg", "Gb"])
            sub_exp("d6", "clampv", lambda: g["bcs"][:], lambda: g["Mc"][:], ["bcs", "Mc"], scale=-1.0, op=ALU.add)
            V(lambda e: e.tensor_copy(out=m_in[:, 0, :], in_=m_in[:, 8, :]), r=["m_in"], w=["m_in"])

            if mix_level < 3:
                return
            P01a = Pacc[0][:, 0:260].rearrange("p (h e) -> p h e", h=2)
            P01b = Pacc[0][:, 512:772].rearrange("p (h e) -> p h e", h=2)
            P23a = Pacc[1][:, 0:260].rearrange("p (h e) -> p h e", h=2)
            P23b = Pacc[1][:, 512:772].rearrange("p (h e) -> p h e", h=2)
            PN = [P01a, P01a, P01b, P01b]
            PNK = ["P01a", "P01a", "P01b", "P01b"]
            PC = [P23a, P23a, P23b, P23b]
            PCK = ["P23a", "P23a", "P23b", "P23b"]
            T0v = Tbv[0][:, 0:4, :]
            T1v = Tbv[1][:, 0:4, :]
            M1v4 = Mb[1][:, :].rearrange("p (h n) -> p h n", h=4)
            for c in range(8):
                sl = c % 2
                cs = slice(c * 128, (c + 1) * 128)
                ms_ = mst[:, c, :]
                mk = ("mst", c)
                G(lambda e, c=c: e.tensor_tensor(out=Cbf[:, :, 0:129], in0=Cst[:, :, 0:129],
                                                 in1=bc3(g["cscale"][:, c, :], 129), op=ALU.mult),
                  r=["Cst", "cscale"], w=["Cbf"])
                for hh in range(4):
                    PE(lambda e, hh=hh, cs=cs: e.matmul(M0v4[:, hh, :], lhsT=qkT[:, 4 + hh, cs], rhs=qkT[:, hh, cs],
                                                        start=True, stop=True),
                       r=[("qkT", 4 + hh), ("qkT", hh)] + RW, w=["M0"], inc=(hh == 3))
                V(lambda e, sl=sl: e.tensor_tensor(out=qks[sl], in0=M0v4, in1=maskS[:].unsqueeze(1).to_broadcast([128, 4, 128]),
                                                   op=ALU.mult), r=["M0", "maskS"] + RW, w=[("qks", sl)])
                for hh in range(4):
                    PE(lambda e, hh=hh, cs=cs: e.transpose(out=T0v[:, hh, :], in_=qkT[:, 4 + hh, cs], identity=ident_b[:]),
                       r=[("qkT", 4 + hh), "ident_b"] + RW, w=["T0"], inc=(hh == 3))
                V(lambda e, sl=sl, c=c: e.tensor_tensor(out=ke[sl], in0=T0v, in1=bc3(g["estate"][:, c, :], 128), op=ALU.mult),
                  r=["T0", "estate"] + RW, w=[("ke", sl)])
                G(lambda e, sl=sl, c=c: e.tensor_tensor(out=vp[sl][:, :, 0:129], in0=vext[:, c, :, 0:129],
                                                        in1=bc3(g["ehat"][:, c, :], 129), op=ALU.mult),
                  r=[("vext", c), "vext1", "ehat"] + RW, w=[("vp", sl)])
                for hh in range(4):
                    last = (hh % 2 == 1)
                    PE(lambda e, hh=hh, cs=cs: e.matmul(PN[hh][:, hh % 2, 0:129], lhsT=qkT[:, hh, cs], rhs=Cbf[:, hh, 0:129],
                                                        start=True, stop=False),
                       r=[("qkT", hh), "Cbf"] + RW, w=[PNK[hh]], inc=False)
                    PE(lambda e, hh=hh, sl=sl: e.matmul(PN[hh][:, hh % 2, 0:129], lhsT=qks[sl][:, hh, :], rhs=vp[sl][:, hh, 0:129],
                                                        start=False, stop=True),
                       r=[("qks", sl), ("vp", sl)] + RW, w=[PNK[hh]], inc=last)
                for hh in range(4):
                    last = (hh % 2 == 1)
                    PE(lambda e, hh=hh, sl=sl, c=c: e.matmul(PC[hh][:, hh % 2, 0:129], lhsT=ke[sl][:, hh, :],
                                                             rhs=vext[:, c, hh, 0:129], start=True, stop=True),
                       r=[("ke", sl), ("vext", c), "vext1"] + RW, w=[PCK[hh]], inc=last)
                V(lambda e: e.tensor_copy(out=ms_[:, 12:14], in_=P01a[:, :, 128]), r=["P01a"], w=[mk])
                V(lambda e: e.tensor_copy(out=ms_[:, 14:16], in_=P01b[:, :, 128]), r=["P01b"], w=[mk])
                V(lambda e: e.scalar_tensor_tensor(out=ms_[:, 0:4], in0=ms_[:, 12:16], scalar=-1.0, in1=ms_[:, 12:16],
                                                   op0=ALU.mult, op1=ALU.max), r=[mk], w=[mk])
                V(lambda e, c=c: e.tensor_tensor(out=ms_[:, 0:4], in0=ms_[:, 0:4], in1=g["clampv"][:, c, :], op=ALU.max),
                  r=[mk, "clampv"], w=[mk])
                V(lambda e: e.tensor_scalar(out=ms_[:, 0:4], in0=ms_[:, 0:4], scalar1=1e-30, scalar2=None, op0=ALU.max),
                  r=[mk], w=[mk])
                V(lambda e: e.reciprocal(out=ms_[:, 4:8], in_=ms_[:, 0:4]), r=[mk], w=[mk])
                for hh in range(4):
                    A(lambda e, hh=hh, sl=sl: e.activation(out=xnb[sl][:, hh * 128:(hh + 1) * 128], in_=PN[hh][:, hh % 2, 0:128],
                                                           func=AF.Square, accum_out=ms_[:, 8 + hh: 9 + hh]),
                      r=[PNK[hh]], w=[("xnb", sl), ("mssq", c)])
                V(lambda e: e.tensor_tensor(out=ms_[:, 12:16], in0=ms_[:, 4:8], in1=ms_[:, 4:8], op=ALU.mult), r=[mk], w=[mk])
                V(lambda e: e.tensor_tensor(out=ms_[:, 12:16], in0=ms_[:, 12:16], in1=ms_[:, 8:12], op=ALU.mult),
                  r=[mk, ("mssq", c)], w=[mk])
                V(lambda e: e.tensor_scalar(out=ms_[:, 12:16], in0=ms_[:, 12:16], scalar1=1.0 / 128, scalar2=EPS,
                                            op0=ALU.mult, op1=ALU.add), r=[mk], w=[mk])
                A(lambda e: e.activation(out=ms_[:, 0:4], in_=ms_[:, 12:16], func=AF.Sqrt), r=[mk], w=[mk])
                V(lambda e: e.reciprocal(out=ms_[:, 12:16], in_=ms_[:, 0:4]), r=[mk], w=[mk])
                V(lambda e: e.tensor_tensor(out=ms_[:, 0:4], in0=ms_[:, 12:16], in1=ms_[:, 4:8], op=ALU.mult), r=[mk], w=[mk])
                for hh in range(4):
                    V(lambda e, hh=hh, sl=sl, c=c: e.scalar_tensor_tensor(
                        out=hmb[sl][:, hh, :], in0=PN[hh][:, hh % 2, 0:128], scalar=ms_[:, hh:hh + 1],
                        in1=og[:, c, hh * 128:(hh + 1) * 128], op0=ALU.mult, op1=ALU.mult),
                      r=[PNK[hh], mk, ("og", c)] + RW, w=[("hmb", sl)])
                for hh in range(4):
                    PE(lambda e, hh=hh, sl=sl: e.transpose(out=T1v[:, hh, :], in_=hmb[sl][:, hh, :], identity=ident_b[:]),
                       r=[("hmb", sl), "ident_b"] + RW, w=["T1"], inc=(hh == 3))
                A(lambda e, cs=cs: e.activation(out=xnT[:, 0:4, cs], in_=T1v, func=AF.Copy), r=["T1"], w=[("xnT", c)])
                V(lambda e, c=c: e.tensor_tensor(out=tmpC[:, 0:2, 0:129], in0=P23a[:, :, 0:129],
                                                 in1=bc3(g["sloc"][:, c, 0:2], 129), op=ALU.mult),
                  r=["P23a", "sloc"] + RW, w=["tmpC"])
                V(lambda e, c=c: e.tensor_tensor(out=tmpC[:, 2:4, 0:129], in0=P23b[:, :, 0:129],
                                                 in1=bc3(g["sloc"][:, c, 2:4], 129), op=ALU.mult),
                  r=["P23b", "sloc"] + RW, w=["tmpC"])
                G(lambda e, c=c: e.tensor_tensor(out=Cst[:, :, 0:129], in0=Cst[:, :, 0:129],
                                                 in1=bc3(g["sprev"][:, c, :], 129), op=ALU.mult),
                  r=["Cst", "sprev", "Cbf"], w=["Cst"])
                G(lambda e: e.tensor_tensor(out=Cst[:, :, 0:129], in0=Cst[:, :, 0:129], in1=tmpC[:, :, 0:129], op=ALU.add),
                  r=["Cst", "tmpC"] + RW, w=["Cst"])
                for gg_ in range(4):
                    PE(lambda e, gg_=gg_, c=c: e.matmul(M1v4[:, gg_, :], lhsT=vg[:, c, gg_ * 128:(gg_ + 1) * 128],
                                                        rhs=wsT[:, gg_, :], start=True, stop=True),
                       r=[("vg", c), "wsT"] + RW, w=["M1"], inc=(gg_ == 3))
                V(lambda e, sl=sl: e.tensor_tensor(out=scr[sl][:, 0:512].rearrange("p (g t) -> p g t", g=4), in0=M1v4,
                                                   in1=bsp[:], op=ALU.add), r=["M1", "bsp"], w=[("scr", sl)])
                G(lambda e, sl=sl, cs=cs: e.tensor_tensor(out=xnT[:, 4:8, cs], in0=scr[sl][:, 0:512].rearrange("p (g t) -> p g t", g=4),
                                                          in1=uT[:, :, cs], op=ALU.mult),
                  r=[("scr", sl)] + [("uT", j_) for j_ in range(4)] + RW, w=[("xnT", c)])
            if mix_level < 4:
                return
            for tt in range(NTT):
                j = tt % 2
                P, PK = Pacc[j], PaccK[j]
                for mc in range(8):
                    PE(lambda e, P=P, mc=mc, tt=tt: e.matmul(P[:, 0:512], lhsT=xnT[:, mc, tt * 128:(tt + 1) * 128],
                                                             rhs=wout[:, mc, 0:512], start=(mc == 0), stop=(mc == 7)),
                       r=[("xnT", tt), "wout", "Wg"], w=[PK[0]], inc=False)
                    PE(lambda e, P=P, mc=mc, tt=tt: e.matmul(P[:, 512:1024], lhsT=xnT[:, mc, tt * 128:(tt + 1) * 128],
                                                             rhs=wout[:, mc, 512:1024], start=(mc == 0), stop=(mc == 7)),
                       r=[("xnT", tt), "wout", "Wg"], w=[PK[1]], inc=(mc == 7))
                epilogue(P, PK, tt, 1)

        def ple(hf):
            load_gpost(1, 7, 1.0)
            for tt in range(NTT):
                prenorm(tt, 3)
            fence("Rg")
            for tt in range(NTT):
                sl = tt % 2
                row0 = hf * TH + tt * 128
                S.dma("sp", lambda e, sl=sl, row0=row0: e.dma_start(out=pt[sl][:], in_=p[row0:row0 + 128, :]),
                      writes=[("pt", sl)])
                V(lambda e, sl=sl: e.tensor_copy(out=ptb[sl][:], in_=pt[sl][:]), r=[("pt", sl)], w=[("ptb", sl)])
                for k in range(2):
                    PE(lambda e, k=k, sl=sl: e.transpose(out=Tbv[sl][:, k, :], in_=ptb[sl][:, k * 128:(k + 1) * 128],
                                                         identity=ident_b[:]),
                       r=[("ptb", sl), "ident_b"], w=[TbK[sl]], inc=(k == 1))
                A(lambda e, sl=sl, tt=tt: e.activation(out=pTs[:, :, tt * 128:(tt + 1) * 128], in_=Tbv[sl][:, 0:2, :], func=AF.Copy),
                  r=[TbK[sl]], w=[("pTs", tt)])
            tiles = seq[(hf, "pg")]
            for cg in range(4):
                slot = ring_need(tiles[cg])
                for tt in range(NTT):
                    j = tt % 2
                    P, PK = Mb[j], MbK[j]
                    for k in range(8):
                        PE(lambda e, k=k, P=P, tt=tt, slot=slot: e.matmul(
                            P[:, 0:256], lhsT=xnT[:, k, tt * 128:(tt + 1) * 128], rhs=ring[slot][:, k, :],
                            start=(k == 0), stop=(k == 7)), r=[("ring", slot), ("xnT", tt)], w=[PK], inc=(k == 7))
                    A(lambda e, P=P, tt=tt, cg=cg: e.activation(out=gate[:, tt, cg * 256:(cg + 1) * 256], in_=P[:, 0:256],
                                                                func=AF.Sigmoid), r=[PK, "Rg"], w=[("gate", tt)])
            for tt in range(NTT):
                j = tt % 2
                sl = tt % 2
                P, PK = Pacc[j], PaccK[j]
                for pc in range(2):
                    PE(lambda e, P=P, pc=pc, tt=tt: e.matmul(P[:, 0:512], lhsT=pTs[:, pc, tt * 128:(tt + 1) * 128],
                                                             rhs=wple[:, pc, 0:512], start=(pc == 0), stop=(pc == 1)),
                       r=[("pTs", tt), "wple"], w=[PK[0]], inc=False)
                    PE(lambda e, P=P, pc=pc, tt=tt: e.matmul(P[:, 512:1024], lhsT=pTs[:, pc, tt * 128:(tt + 1) * 128],
                                                             rhs=wple[:, pc, 512:1024], start=(pc == 0), stop=(pc == 1)),
                       r=[("pTs", tt), "wple"], w=[PK[1]], inc=(pc == 1))
                s, sk = newstat()
                V(lambda e, P=P, tt=tt, sl=sl: e.tensor_tensor(out=scr[sl][:], in0=gate[:, tt, :], in1=P[:, :], op=ALU.mult),
                  r=PK + [("gate", tt), "Rg"], w=[("scr", sl)])
                A(lambda e, sl=sl, s=s: e.activation(out=xnb[sl][:], in_=scr[sl][:], func=AF.Square, accum_out=s[:, 0:1]),
                  r=[("scr", sl)], w=[("xnb", sl), sk])
                rstd_from_ssq(s, sk, D)
                V(lambda e, sl=sl, s=s: e.scalar_tensor_tensor(out=scr[sl][:], in0=scr[sl][:], scalar=s[:, 3:4], in1=gpost[1][:],
                                                               op0=ALU.mult, op1=ALU.mult),
                  r=[("scr", sl), sk, ("gpost", 1)], w=[("scr", sl)])
                V(lambda e, tt=tt, sl=sl: e.tensor_tensor(out=h[:, tt, :], in0=h[:, tt, :], in1=scr[sl][:], op=ALU.add),
                  r=[("h", tt), ("scr", sl)], w=[("h", tt)])

        def load_x(hf, tt):
            row0 = hf * TH + tt * 128
            S.dma("sp", lambda e: e.dma_start(out=h[:, tt, :], in_=x[row0:row0 + 128, :]), writes=[("h", tt)])

        def store_y(hf, tt):
            row0 = hf * TH + tt * 128
            S.dma("sp", lambda e: e.dma_start(out=y[row0:row0 + 128, :], in_=h[:, tt, :]), reads=[("h", tt)],
                  writes=[("y", hf, tt)])

        for tt in range(NTT):
            load_x(0, tt)
        ykeys = []
        for hf in range(nhalf):
            ffn(hf, 1, 0, 0, 1)
            if stage >= 2:
                mixer(hf)
            if stage >= 3:
                ffn(hf, 2, 2, 0, 5)
            if stage >= 4:
                ple(hf)
            for tt in range(NTT):
                store_y(hf, tt)
                ykeys.append(("y", hf, tt))
                if hf + 1 < nhalf:
                    load_x(hf + 1, tt)
        S.final_wait("sp", ykeys)
        S.run()
    return nc


_WKEYS = ["ffn1_gu", "ffn1_down", "ffn2_gu", "ffn2_down", "w_in", "conv_w", "conv_b", "b_if", "mh_norm_g",
          "gmlp_ln_g", "gmlp_ln_b", "w_spatial", "b_spatial", "w_out", "w_ple", "w_ple_gate", "norm_g"]


def kernel(**inputs):
    from concourse.bass_utils import run_bass_kernel_spmd
    n = 8
    x = np.asarray(inputs["x"], dtype=np.float32)
    p = np.asarray(inputs["p"], dtype=np.float32)
    shared = {k: np.ascontiguousarray(np.asarray(inputs[k], dtype=np.float32)[0]) for k in _WKEYS}
    shared["ident"] = np.eye(128, dtype=np.float32)
    shared["utm"] = np.triu(np.ones((128, 128), dtype=np.float32))
    in_maps = []
    for b in range(n):
        m = dict(shared)
        m["x"] = np.ascontiguousarray(x[b])
        m["p"] = np.ascontiguousarray(p[0, b])
        in_maps.append(m)
    nc = build(stage=4, nhalf=2)
    res = run_bass_kernel_spmd(nc, in_maps, core_ids=list(range(n)))
    return np.stack([np.asarray(r["y"], dtype=np.float32) for r in res.results], axis=0)
```

```python
import numpy as np
from contextlib import ExitStack
import concourse.bass as bass
import concourse.mybir as mybir

F32 = mybir.dt.float32
BF16 = mybir.dt.bfloat16
AF = mybir.ActivationFunctionType
ALU = mybir.AluOpType
AX = mybir.AxisListType


ENGS = ("pe", "act", "dve", "pool", "sp")


class Sched:
    def __init__(self, nc, es):
        self.nc = nc
        self.es = es
        self.sem = {k: es.enter_context(nc.semaphore("s_" + k)) for k in ENGS}
        self.cnt = {k: 0 for k in ENGS}
        self.seen = {k: {} for k in ENGS}
        self.res = {}
        self.prog = {k: [] for k in ENGS}
        self.dsem = {}
        self.pending_inc = {k: None for k in ENGS}
        self.nwaits = 0
        self.excl = set()

    def _r(self, key):
        r = self.res.get(key)
        if r is None:
            r = {"w": None, "r": []}
            self.res[key] = r
        return r

    def _deps(self, eng, reads, writes):
        deps = {}

        def add(d):
            if d is None:
                return
            k, v = d
            if deps.get(k, 0) < v:
                deps[k] = v

        for key in reads:
            r = self._r(key)
            add(r["w"])
            if key in self.excl:
                for d in r["r"]:
                    if d[0] != eng:
                        add(d)
        for key in writes:
            r = self._r(key)
            add(r["w"])
            for d in r["r"]:
                add(d)
        out = []
        for k, v in deps.items():
            if eng == "pe" and k == "pe":
                continue
            if self.seen[eng].get(k, 0) >= v:
                continue
            self.seen[eng][k] = v
            out.append((k, v))
        return out

    def _semobj(self, k):
        if k in self.sem:
            return self.sem[k]
        return self.dsem[k][0]

    def op(self, eng, fn, reads=(), writes=(), inc=True):
        waits = self._deps(eng, reads, writes)
        self.nwaits += len(waits)
        if inc:
            self.cnt[eng] += 1
            stamp = (eng, self.cnt[eng])
        else:
            stamp = (eng, self.cnt[eng] + 1)
        for key in reads:
            self._r(key)["r"].append(stamp)
        for key in writes:
            r = self._r(key)
            r["w"] = stamp
            r["r"] = []
        sem = self.sem[eng]
        wl = [(self._semobj(k), v) for k, v in waits]

        def emit(e):
            for s, v in wl:
                e.wait_ge(s, v)
            ins = fn(e)
            if inc:
                ins.then_inc(sem, 1)

        self.prog[eng].append(emit)
        if not inc:
            self.pending_inc[eng] = True
        else:
            self.pending_inc[eng] = None

    def dma(self, eng, fn, reads=(), writes=(), n=1):
        waits = self._deps(eng, reads, writes)
        self.nwaits += len(waits)
        dkey = "d_" + str(writes[0])
        if dkey not in self.dsem:
            nm = "dm%d" % len(self.dsem)
            self.dsem[dkey] = [self.es.enter_context(self.nc.semaphore(nm)), 0]
        ent = self.dsem[dkey]
        ent[1] += 16 * n
        stamp = (dkey, ent[1])
        for key in reads:
            self._r(key)["r"].append(stamp)
        for key in writes:
            r = self._r(key)
            r["w"] = stamp
            r["r"] = []
        dsem = ent[0]
        wl = [(self._semobj(k), v) for k, v in waits]

        def emit(e):
            for s, v in wl:
                e.wait_ge(s, v)
            ins = fn(e)
            if not isinstance(ins, (list, tuple)):
                ins = [ins]
            assert len(ins) == n, (len(ins), n)
            for i in ins:
                i.then_inc(dsem, 16)

        self.prog[eng].append(emit)

    def final_wait(self, eng, keys):
        waits = self._deps(eng, keys, ())
        wl = [(self._semobj(k), v) for k, v in waits]

        def emit(e):
            for s, v in wl:
                e.wait_ge(s, v)

        self.prog[eng].append(emit)

    def run(self):
        nc = self.nc
        with nc.Block() as block:
            @block.tensor
            def _(e):
                for f in self.prog["pe"]:
                    f(e)

            @block.scalar
            def _(e):
                for f in self.prog["act"]:
                    f(e)

            @block.vector
            def _(e):
                for f in self.prog["dve"]:
                    f(e)

            @block.gpsimd
            def _(e):
                for f in self.prog["pool"]:
                    f(e)

            @block.sync
            def _(e):
                for f in self.prog["sp"]:
                    f(e)


T = 2048
TH = 1024
NTT = 8
D = 1024
DFF = 2816
NFC = 22
EPS = 1e-6
NRING = 3
QSCALE = 128 ** -0.5


def build(stage=4, nhalf=2, mix_level=4):
    nc = bass.Bass("TRN2", target_bir_lowering=False)
    dt_in = lambda name, shape: nc.dram_tensor(name, shape, F32, kind="ExternalInput").ap()
    x = dt_in("x", [T, D])
    p = dt_in("p", [T, 256])
    ffn_gu = {1: dt_in("ffn1_gu", [D, 2 * DFF]), 2: dt_in("ffn2_gu", [D, 2 * DFF])}
    ffn_dn = {1: dt_in("ffn1_down", [DFF, D]), 2: dt_in("ffn2_down", [DFF, D])}
    w_in = dt_in("w_in", [D, 3080])
    conv_w = dt_in("conv_w", [4, 1024])
    conv_b = dt_in("conv_b", [1024])
    b_if = dt_in("b_if", [8])
    mh_g = dt_in("mh_norm_g", [512])
    ln_g = dt_in("gmlp_ln_g", [512])
    ln_b = dt_in("gmlp_ln_b", [512])
    w_sp = dt_in("w_spatial", [4, 128, 128])
    b_sp = dt_in("b_spatial", [4, 128])
    w_out = dt_in("w_out", [D, D])
    w_ple = dt_in("w_ple", [256, D])
    w_pg = dt_in("w_ple_gate", [D, D])
    norm_g = dt_in("norm_g", [8, D])
    ident = dt_in("ident", [128, 128])
    utm = dt_in("utm", [128, 128])
    y = nc.dram_tensor("y", [T, D], F32, kind="ExternalOutput").ap()

    with ExitStack() as es:
        S = Sched(nc, es)
        sb = lambda name, shape, dt: es.enter_context(nc.sbuf_tensor(name, shape, dt))
        ps = lambda name, shape, dt: es.enter_context(nc.psum_tensor(name, shape, dt))

        def V(fn, r=(), w=()):
            S.op("dve", fn, reads=r, writes=w)

        def A(fn, r=(), w=()):
            S.op("act", fn, reads=r, writes=w)

        def G(fn, r=(), w=()):
            S.op("dve", fn, reads=r, writes=w)

        def PE(fn, r=(), w=(), inc=True):
            S.op("pe", fn, reads=r, writes=w, inc=inc)

        h = sb("h", [128, NTT, D], F32)
        xnT = sb("xnT", [128, 8, TH], BF16)
        R = sb("R", [128, 22528], BF16)
        WD = sb("WD", [128, 22528], BF16)
        ring = [sb("ring%d" % i, [128, 8, 256], BF16) for i in range(NRING)]
        wple = sb("wple", [128, 2, D], BF16)
        pTs = sb("pTs", [128, 2, TH], BF16)
        gpost = [sb("gpost%d" % i, [128, D], F32) for i in range(2)]
        gcol = sb("gcol", [128, 4, 8], F32)
        xnb = [sb("xnb%d" % i, [128, D], BF16) for i in range(2)]
        scr = [sb("scr%d" % i, [128, D], F32) for i in range(2)]
        sg = [sb("sg%d" % i, [128, 512], F32) for i in range(2)]
        stt = sb("stt", [128, 16, 8], F32)
        ident_f = sb("ident_f", [128, 128], F32)
        ident_b = sb("ident_b", [128, 128], BF16)
        ut_f = sb("ut_f", [128, 128], F32)
        ones_f = sb("ones_f", [128, 128], F32)
        maskS = sb("maskS", [128, 128], F32)
        cw = sb("cw", [128, 4, 8], F32)
        cb = sb("cb", [128, 8], F32)
        bif = sb("bif", [128, 8], F32)
        mhg = sb("mhg", [128, 512], F32)
        lng = sb("lng", [128, 512], F32)
        lnb = sb("lnb", [128, 512], F32)
        wsT = sb("wsT", [128, 4, 128], BF16)
        bsp = sb("bsp", [128, 4, 128], F32)
        wif = sb("wif", [128, 8, 8], BF16)
        Cst = sb("Cst", [128, 4, 130], F32)
        Cbf = sb("Cbf", [128, 4, 130], BF16)
        carry = sb("carry", [128, 8, 4], F32)
        pt = [sb("pt%d" % i, [128, 256], F32) for i in range(2)]
        ptb = [sb("ptb%d" % i, [128, 256], BF16) for i in range(2)]
        fz = sb("fz", [128, 4], F32)
        junk = sb("junk", [128, D], BF16)
        gnames = ["gat8", "li", "fpv", "gab", "ge1", "gl1", "gmn", "lf", "bcs", "abc", "gg", "Gb", "mloc", "amp",
                  "d1", "sprev", "d2", "sloc", "Mc", "d3", "cscale", "d4", "ehat", "d5", "estate", "d6", "clampv"]
        GA = {}
        for nm in gnames:
            GA[nm] = sb("g_" + nm, [128, 8, 8 if nm == "gat8" else 4], F32)
        m_in = sb("m_in", [128, 9, 4], F32)
        gmax = sb("gmax", [128, 1], F32)
        dg = sb("dg", [128, 32], F32)
        ggpad = sb("ggpad", [128, 128], F32)
        mst = sb("mst", [128, 8, 16], F32)

        hT = R[:, :].rearrange("p (f t) -> p f t", f=NFC)
        qkT = R[:, 0:8192].rearrange("p (j t) -> p j t", j=8)
        og = R[:, 8192:12288].rearrange("p (c e) -> p c e", c=8)
        uT = R[:, 12288:16384].rearrange("p (j t) -> p j t", j=4)
        vg = R[:, 16384:20480].rearrange("p (c e) -> p c e", c=8)
        gate = R[:, 0:16384].bitcast(F32).rearrange("p (c e) -> p c e", c=8)
        Wdn = WD[:, :].rearrange("p (f n) -> p f n", f=NFC)
        wout = WD[:, 0:8192].rearrange("p (m n) -> p m n", m=8)
        vext = WD[:, 8192:12352].rearrange("p (c h e) -> p c h e", c=8, h=4)
        stg = [WD[:, 12352 + i * 2056: 12352 + (i + 1) * 2056].bitcast(F32) for i in range(2)]
        o0 = 16464
        ke = [WD[:, o0 + i * 512: o0 + (i + 1) * 512].rearrange("p (h d) -> p h d", h=4) for i in range(2)]
        o0 += 1024
        vp = [WD[:, o0 + i * 520: o0 + (i + 1) * 520].rearrange("p (h e) -> p h e", h=4) for i in range(2)]
        o0 += 1040
        qks = [WD[:, o0 + i * 512: o0 + (i + 1) * 512].rearrange("p (h d) -> p h d", h=4) for i in range(2)]
        o0 += 1024
        hmb = [WD[:, o0 + i * 512: o0 + (i + 1) * 512].rearrange("p (h d) -> p h d", h=4) for i in range(2)]
        o0 += 1024
        tmpC = WD[:, o0: o0 + 1040].bitcast(F32).rearrange("p (h e) -> p h e", h=4)
        o0 += 1040
        assert o0 <= 22528

        Pacc = [ps("P01", [128, 1024], F32), ps("P23", [128, 1024], F32)]
        PaccK = [["P01a", "P01b"], ["P23a", "P23b"]]
        Tb = [ps("T0", [128, 512], F32), ps("T1", [128, 512], F32)]
        TbK = ["T0", "T1"]
        Mb = [ps("M0", [128, 512], F32), ps("M1", [128, 512], F32)]
        MbK = ["M0", "M1"]
        S.excl = {"P01a", "P01b", "P23a", "P23b", "T0", "T1", "M0", "M1"}
        Tbv = [t[:, :].bitcast(BF16).rearrange("p (k n) -> p k n", k=8) for t in Tb]

        st_i = [0]

        def newstat():
            st_i[0] = (st_i[0] + 1) % 16
            return stt[:, st_i[0], :], ("st", st_i[0])

        def ld(eng, out_ap, in_ap, key, slow=False):
            S.dma(eng, lambda e: e.dma_start(out=out_ap, in_=in_ap, allow_slow_non_contiguous=slow), writes=[key])

        ld("sp", ident_f[:], ident, "ident_f")
        ld("sp", ut_f[:], utm, "ut_f")
        for a_ in range(4):
            ld("sp", gcol[:, a_, :], norm_g[2 * a_].rearrange("(k p) -> p k", p=128), "gcol", slow=True)
        for k_ in range(4):
            ld("sp", cw[:, k_, :], conv_w[k_].rearrange("(j p) -> p j", p=128), "cw", slow=True)
        ld("sp", cb[:], conv_b.rearrange("(j p) -> p j", p=128), "cb", slow=True)
        ld("sp", bif[:], b_if.partition_broadcast(128), "bif")
        ld("sp", mhg[:], mh_g.partition_broadcast(128), "mhg")
        ld("sp", lng[:], ln_g.partition_broadcast(128), "lng")
        ld("sp", lnb[:], ln_b.partition_broadcast(128), "lnb")
        ld("sp", bsp[:], b_sp.rearrange("g t -> (g t)").partition_broadcast(128), "bsp")
        ld("sp", scr[0][:, 0:512].rearrange("p (g s) -> p g s", g=4), w_sp.rearrange("g t s -> t g s"), ("scr", 0))
        S.dma("pool", lambda e: e.dma_start(out=wif[:], in_=w_in[:, 2048:2056].rearrange("(k p) n -> p k n", p=128)),
              writes=["wif"])
        S.dma("pool", lambda e: e.dma_start(out=wple[:], in_=w_ple.rearrange("(k p) n -> p k n", p=128)),
              writes=["wple"])

        V(lambda e: e.tensor_copy(out=ident_b[:], in_=ident_f[:]), r=["ident_f"], w=["ident_b"])
        V(lambda e: e.memset(ones_f[:], 1.0), w=["ones_f"])
        V(lambda e: e.tensor_scalar(out=maskS[:], in0=ut_f[:], scalar1=QSCALE, scalar2=None, op0=ALU.mult),
          r=["ut_f"], w=["maskS"])
        V(lambda e: e.memset(Cst[:], 0.0), w=["Cst"])
        V(lambda e: e.memset(carry[:], 0.0), w=["carry"])
        V(lambda e: e.memset(m_in[:], 0.0), w=["m_in"])
        V(lambda e: e.memset(fz[:], 0.0), w=["fz"])
        V(lambda e: e.memset(ggpad[:], 0.0), w=["ggpad"])
        V(lambda e: e.memset(dg[:], 0.0), w=["dg"])
        M0v4 = Mb[0][:, :].rearrange("p (h n) -> p h n", h=4)
        for g_ in range(4):
            PE(lambda e, g_=g_: e.transpose(out=M0v4[:, g_, :], in_=scr[0][:, g_ * 128:(g_ + 1) * 128], identity=ident_f[:]),
               r=[("scr", 0), "ident_f"], w=["M0"], inc=(g_ == 3))
        V(lambda e: e.tensor_tensor(out=wsT[:], in0=M0v4, in1=ut_f[:].unsqueeze(1).to_broadcast([128, 4, 128]), op=ALU.mult),
          r=["M0", "ut_f"], w=["wsT"])

        ring_specs = []

        def ring_add(srcs):
            ring_specs.append(srcs)
            return len(ring_specs) - 1

        ring_issued = [0]

        def ring_need(i):
            while ring_issued[0] < len(ring_specs) and ring_issued[0] <= i + NRING - 1:
                j = ring_issued[0]
                slot = j % NRING
                srcs = ring_specs[j]

                def fn(e, srcs=srcs, slot=slot):
                    out = []
                    for (c0, c1, src) in srcs:
                        out.append(e.dma_start(out=ring[slot][:, :, c0:c1], in_=src.rearrange("(k p) n -> p k n", p=128)))
                    return out

                S.dma("pool", fn, writes=[("ring", slot)], n=len(srcs))
                ring_issued[0] += 1
            return i % NRING

        seq = {}
        for hf in range(nhalf):
            for f in (1, 2):
                if f == 1 or stage >= 3:
                    seq[(hf, "gu", f)] = [ring_add([(0, 128, ffn_gu[f][:, fc * 128:(fc + 1) * 128]),
                                                     (128, 256, ffn_gu[f][:, DFF + fc * 128: DFF + (fc + 1) * 128])])
                                          for fc in range(NFC)]
                if f == 1 and stage >= 2:
                    cols = [1024, 1280, 1536, 1792, 2568, 2824, 0, 256, 512, 768, 2056, 2312]
                    seq[(hf, "win")] = [ring_add([(0, 256, w_in[:, c:c + 256])]) for c in cols]
            if stage >= 4:
                seq[(hf, "pg")] = [ring_add([(0, 256, w_pg[:, c * 256:(c + 1) * 256])]) for c in range(4)]

        fence_n = [0]

        def fence(key):
            fence_n[0] += 1
            V(lambda e: e.memset(fz[:, 0:1], 0.0), w=[key, "fzs"])

        def load_gpost(slot, idx, scale):
            S.dma("sp", lambda e: e.dma_start(out=gpost[slot][:], in_=norm_g[idx].partition_broadcast(128)),
                  writes=[("gpost", slot)])
            if scale != 1.0:
                V(lambda e: e.tensor_scalar(out=gpost[slot][:], in0=gpost[slot][:], scalar1=scale, scalar2=None, op0=ALU.mult),
                  r=[("gpost", slot)], w=[("gpost", slot)])

        def rstd_from_ssq(s, sk, n, col_in=0):
            V(lambda e: e.tensor_scalar(out=s[:, 1:2], in0=s[:, col_in:col_in + 1], scalar1=1.0 / n, scalar2=EPS,
                                        op0=ALU.mult, op1=ALU.add), r=[sk], w=[sk])
            A(lambda e: e.activation(out=s[:, 2:3], in_=s[:, 1:2], func=AF.Sqrt), r=[sk], w=[sk])
            V(lambda e: e.reciprocal(out=s[:, 3:4], in_=s[:, 2:3]), r=[sk], w=[sk])

        def prenorm_stats(tt, gi):
            s, sk = newstat()
            sl = tt % 2
            A(lambda e: e.activation(out=junk[:], in_=h[:, tt, :], func=AF.Square, accum_out=s[:, 0:1]),
              r=[("h", tt)], w=[sk])
            rstd_from_ssq(s, sk, D)
            A(lambda e: e.activation(out=xnb[sl][:], in_=h[:, tt, :], func=AF.Copy, scale=s[:, 3:4]),
              r=[("h", tt), sk], w=[("xnb", sl)])

        def prenorm_T(tt, gi):
            sl = tt % 2
            for k in range(8):
                PE(lambda e, k=k: e.transpose(out=Tbv[sl][:, k, :], in_=xnb[sl][:, k * 128:(k + 1) * 128], identity=ident_b[:]),
                   r=[("xnb", sl), "ident_b"], w=[TbK[sl]], inc=(k == 7))
            V(lambda e: e.tensor_tensor(out=xnT[:, :, tt * 128:(tt + 1) * 128], in0=Tbv[sl],
                                        in1=gcol[:, gi, :].unsqueeze(2).to_broadcast([128, 8, 128]), op=ALU.mult),
              r=[TbK[sl], "gcol"], w=[("xnT", tt)])

        def prenorm(tt, gi):
            prenorm_stats(tt, gi)
            prenorm_T(tt, gi)

        def tail_loop(group_fn, epi_fn, nxt, lag=1):
            for tt in range(NTT):
                group_fn(tt)
                if nxt is not None and tt >= lag:
                    prenorm_T(tt - lag, nxt)
                epi_fn(tt)
                if nxt is not None:
                    prenorm_stats(tt, nxt)
            if nxt is not None:
                for tt in range(NTT - lag, NTT):
                    prenorm_T(tt, nxt)

        def epilogue(P, PK, tt, gslot):
            s, sk = newstat()
            sl = tt % 2
            A(lambda e: e.activation(out=junk[:], in_=P[:, :], func=AF.Square, accum_out=s[:, 0:1]),
              r=PK, w=[sk])
            rstd_from_ssq(s, sk, D)
            V(lambda e: e.scalar_tensor_tensor(out=scr[sl][:], in0=P[:, :], scalar=s[:, 3:4], in1=gpost[gslot][:],
                                               op0=ALU.mult, op1=ALU.mult),
              r=PK + [sk, ("gpost", gslot)], w=[("scr", sl)])
            V(lambda e: e.tensor_tensor(out=h[:, tt, :], in0=h[:, tt, :], in1=scr[sl][:], op=ALU.add),
              r=[("h", tt), ("scr", sl)], w=[("h", tt)])

        def ffn(hf, f, gslot, gidx, pre_gi, nxt):
            load_gpost(gslot, gidx, 0.5)
            if pre_gi is not None:
                for tt in range(NTT):
                    prenorm(tt, pre_gi)
            fence("Rg")
            fence("Wg")
            tiles = seq[(hf, "gu", f)]
            for fc in range(NFC):
                slot = ring_need(tiles[fc])
                S.dma("pool", lambda e, fc=fc: e.dma_start(out=Wdn[:, fc, :], in_=ffn_dn[f][fc * 128:(fc + 1) * 128, :]),
                      reads=["Wg"], writes=[("Wdn", fc)])
                for tb in range(2):
                    j = (fc * 2 + tb) % 2
                    P, PK = Pacc[j], PaccK[j]
                    xk = [("xnT", tb * 4 + i) for i in range(4)]
                    for k in range(8):
                        PE(lambda e, k=k, P=P, slot=slot, tb=tb: e.matmul(
                            P[:, 0:512], lhsT=ring[slot][:, k, 0:128], rhs=xnT[:, k, tb * 512:(tb + 1) * 512],
                            start=(k == 0), stop=(k == 7)), r=[("ring", slot)] + xk, w=[PK[0]], inc=False)
                    for k in range(8):
                        PE(lambda e, k=k, P=P, slot=slot, tb=tb: e.matmul(
                            P[:, 512:1024], lhsT=ring[slot][:, k, 128:256], rhs=xnT[:, k, tb * 512:(tb + 1) * 512],
                            start=(k == 0), stop=(k == 7)), r=[("ring", slot)] + xk, w=[PK[1]], inc=(k == 7))
                    A(lambda e, P=P, j=j: e.activation(out=sg[j][:], in_=P[:, 0:512], func=AF.Silu),
                      r=[PK[0]], w=[("sg", j)])
                    V(lambda e, P=P, j=j, fc=fc, tb=tb: e.tensor_tensor(
                        out=hT[:, fc, tb * 512:(tb + 1) * 512], in0=sg[j][:], in1=P[:, 512:1024], op=ALU.mult),
                      r=[("sg", j), PK[1], "Rg"], w=[("hT", tb)])
            def dn_group(tt):
                j = tt % 2
                P = Pacc[j]
                PK = PaccK[j]
                for fc in range(NFC):
                    PE(lambda e, P=P, fc=fc, tt=tt: e.matmul(
                        P[:, 0:512], lhsT=hT[:, fc, tt * 128:(tt + 1) * 128], rhs=Wdn[:, fc, 0:512],
                        start=(fc == 0), stop=(fc == NFC - 1)),
                       r=[("hT", tt // 4), ("Wdn", fc), "Rg", "Wg"], w=[PK[0]], inc=False)
                    PE(lambda e, P=P, fc=fc, tt=tt: e.matmul(
                        P[:, 512:1024], lhsT=hT[:, fc, tt * 128:(tt + 1) * 128], rhs=Wdn[:, fc, 512:1024],
                        start=(fc == 0), stop=(fc == NFC - 1)),
                       r=[("hT", tt // 4), ("Wdn", fc), "Rg", "Wg"], w=[PK[1]], inc=(fc == NFC - 1))

            tail_loop(dn_group, lambda tt: epilogue(Pacc[tt % 2], PaccK[tt % 2], tt, gslot), nxt)

        def bc3(ap2, n):
            return ap2.unsqueeze(2).to_broadcast([128, ap2.shape[1], n])

        def mixer(hf, pre_gi, nxt):
            load_gpost(1, 3, 1.0)
            if pre_gi is not None:
                for tt in range(NTT):
                    prenorm(tt, pre_gi)
            fence("Rg")
            fence("Wg")
            S.dma("pool", lambda e: e.dma_start(out=wout, in_=w_out.rearrange("(k p) n -> p k n", p=128)),
                  reads=["Wg"], writes=["wout"])
            V(lambda e: e.memset(vext[:, :, :, 128:130], 1.0), r=["Wg"], w=["vext1"])
            tiles = seq[(hf, "win")]
            RW = ["Rg", "Wg"]

            def tokproj(t0, evac):
                s0 = ring_need(tiles[t0])
                s1 = tiles[t0 + 1] % NRING
                for tt in range(NTT):
                    j = tt % 2
                    P, PK = Pacc[j], PaccK[j]
                    for half, sl_ in ((0, s0), (1, s1)):
                        for k in range(8):
                            PE(lambda e, k=k, P=P, sl_=sl_, half=half, tt=tt: e.matmul(
                                P[:, half * 256:(half + 1) * 256], lhsT=xnT[:, k, tt * 128:(tt + 1) * 128],
                                rhs=ring[sl_][:, k, :], start=(k == 0), stop=(k == 7)),
                               r=[("ring", sl_), ("xnT", tt)], w=[PK[0]], inc=(k == 7 and half == 1))
                    evac(tt, P, PK)

            def evac_v(tt, P, PK):
                A(lambda e: e.activation(out=vext[:, tt, :, 0:128], in_=P[:, 0:512].rearrange("p (h e) -> p h e", h=4),
                                         func=AF.Copy), r=[PK[0]] + RW, w=[("vext", tt)])
                for k in range(8):
                    PE(lambda e, k=k: e.matmul(Mb[0][:, 0:8], lhsT=xnT[:, k, tt * 128:(tt + 1) * 128], rhs=wif[:, k, :],
                                               start=(k == 0), stop=(k == 7)),
                       r=["wif", ("xnT", tt)], w=["M0"], inc=(k == 7))
                V(lambda e: e.tensor_copy(out=GA["gat8"][:, tt, :], in_=Mb[0][:, 0:8]), r=["M0"], w=["gat8"])

            def evac_o(tt, P, PK):
                sl = tt % 2
                A(lambda e: e.activation(out=sg[sl][:], in_=P[:, 0:512], func=AF.Sigmoid), r=[PK[0]], w=[("sg", sl)])
                G(lambda e: e.tensor_tensor(out=og[:, tt, :], in0=sg[sl][:], in1=mhg[:], op=ALU.mult),
                  r=[("sg", sl), "mhg"] + RW, w=[("og", tt)])

            def evac_vg(tt, P, PK):
                sl = tt % 2
                s, sk = newstat()
                A(lambda e: e.activation(out=sg[sl][:], in_=P[:, 0:512], func=AF.Gelu), r=[PK[0]], w=[("sg", sl)])
                V(lambda e: e.bn_stats(out=s[:, 0:6], in_=sg[sl][:]), r=[("sg", sl)], w=[sk])
                s2, sk2 = newstat()
                V(lambda e: e.bn_aggr(out=s2[:, 4:6], in_=s[:, 0:6]), r=[sk], w=[sk2])
                V(lambda e: e.tensor_scalar(out=s2[:, 1:2], in0=s2[:, 5:6], scalar1=EPS, scalar2=None, op0=ALU.add),
                  r=[sk2], w=[sk2])
                A(lambda e: e.activation(out=s2[:, 2:3], in_=s2[:, 1:2], func=AF.Sqrt), r=[sk2], w=[sk2])
                V(lambda e: e.reciprocal(out=s2[:, 3:4], in_=s2[:, 2:3]), r=[sk2], w=[sk2])
                V(lambda e: e.tensor_scalar(out=scr[sl][:, 0:512], in0=sg[sl][:], scalar1=s2[:, 4:5], scalar2=s2[:, 3:4],
                                            op0=ALU.subtract, op1=ALU.mult), r=[("sg", sl), sk2], w=[("scr", sl)])
                G(lambda e: e.tensor_tensor(out=scr[sl][:, 0:512], in0=scr[sl][:, 0:512], in1=lng[:], op=ALU.mult),
                  r=[("scr", sl), "lng"], w=[("scr", sl)])
                G(lambda e: e.tensor_tensor(out=vg[:, tt, :], in0=scr[sl][:, 0:512], in1=lnb[:], op=ALU.add),
                  r=[("scr", sl), "lnb"] + RW, w=[("vg", tt)])

            tokproj(0, evac_v)
            tokproj(2, evac_o)
            tokproj(4, evac_vg)

            def featproj(ti, evac):
                slot = ring_need(tiles[ti])
                for cc in range(2):
                    for tb in range(2):
                        j = (cc * 2 + tb) % 2
                        P, PK = Mb[j], MbK[j]
                        xk = [("xnT", tb * 4 + i) for i in range(4)]
                        for k in range(8):
                            PE(lambda e, k=k, P=P, cc=cc, tb=tb: e.matmul(
                                P[:, :], lhsT=ring[slot][:, k, cc * 128:(cc + 1) * 128],
                                rhs=xnT[:, k, tb * 512:(tb + 1) * 512], start=(k == 0), stop=(k == 7)),
                               r=[("ring", slot)] + xk, w=[PK], inc=(k == 7))
                        evac(cc, tb, P, PK)

            def qk_evac(jbase):
                def ev(cc, tb, P, PK):
                    jj = jbase + cc
                    sl = jj % 2
                    if tb == 0:
                        V(lambda e: e.tensor_copy(out=stg[sl][:, 0:3], in_=carry[:, jj, 0:3]),
                          r=["carry"] + RW, w=[("stg", sl)])
                    A(lambda e: e.activation(out=stg[sl][:, 3 + tb * 512: 3 + (tb + 1) * 512], in_=P[:, :], func=AF.Copy),
                      r=[PK] + RW, w=[("stg", sl)])
                    if tb == 1:
                        acc = scr[sl]
                        V(lambda e: e.tensor_scalar(out=acc[:], in0=stg[sl][:, 3:1027], scalar1=cw[:, 3, jj:jj + 1],
                                                    scalar2=cb[:, jj:jj + 1], op0=ALU.mult, op1=ALU.add),
                          r=[("stg", sl), "cw", "cb"] + RW, w=[("scr", sl)])
                        for kk in (2, 1, 0):
                            V(lambda e, kk=kk: e.scalar_tensor_tensor(out=acc[:], in0=stg[sl][:, kk:kk + 1024],
                                                                      scalar=cw[:, kk, jj:jj + 1], in1=acc[:],
                                                                      op0=ALU.mult, op1=ALU.add),
                              r=[("stg", sl), "cw", ("scr", sl)] + RW, w=[("scr", sl)])
                        A(lambda e: e.activation(out=qkT[:, jj, :], in_=acc[:], func=AF.Silu),
                          r=[("scr", sl)] + RW, w=[("qkT", jj)])
                        V(lambda e: e.tensor_copy(out=carry[:, jj, 0:3], in_=stg[sl][:, 1024:1027]),
                          r=[("stg", sl)] + RW, w=["carry"])
                return ev

            def u_evac(jbase):
                def ev(cc, tb, P, PK):
                    jj = jbase + cc
                    A(lambda e: e.activation(out=uT[:, jj, tb * 512:(tb + 1) * 512], in_=P[:, :], func=AF.Gelu),
                      r=[PK] + RW, w=[("uT", jj)])
                return ev

            featproj(6, qk_evac(0))
            featproj(7, qk_evac(2))
            featproj(8, qk_evac(4))
            featproj(9, qk_evac(6))
            featproj(10, u_evac(0))
            featproj(11, u_evac(2))

            if mix_level < 2:
                return
            g = GA
            fl = lambda nm: g[nm][:, :, :].rearrange("p c h -> p (c h)")
            V(lambda e: e.tensor_tensor(out=g["li"][:], in0=g["gat8"][:, :, 0:4],
                                        in1=bif[:, 0:4].unsqueeze(1).to_broadcast([128, 8, 4]), op=ALU.add),
              r=["gat8", "bif"], w=["li"])
            V(lambda e: e.tensor_tensor(out=g["fpv"][:], in0=g["gat8"][:, :, 4:8],
                                        in1=bif[:, 4:8].unsqueeze(1).to_broadcast([128, 8, 4]), op=ALU.add),
              r=["gat8", "bif"], w=["fpv"])
            V(lambda e: e.scalar_tensor_tensor(out=g["gab"][:], in0=g["fpv"][:], scalar=-1.0, in1=g["fpv"][:],
                                               op0=ALU.mult, op1=ALU.min), r=["fpv"], w=["gab"])
            A(lambda e: e.activation(out=g["ge1"][:], in_=g["gab"][:], func=AF.Exp), r=["gab"], w=["ge1"])
            A(lambda e: e.activation(out=g["gl1"][:], in_=g["ge1"][:], func=AF.Ln, bias=1.0), r=["ge1"], w=["gl1"])
            V(lambda e: e.tensor_single_scalar(out=g["gmn"][:], in_=g["fpv"][:], scalar=0.0, op=ALU.min),
              r=["fpv"], w=["gmn"])
            V(lambda e: e.tensor_tensor(out=g["lf"][:], in0=g["gmn"][:], in1=g["gl1"][:], op=ALU.subtract),
              r=["gmn", "gl1"], w=["lf"])
            PE(lambda e: e.matmul(Mb[0][:, 0:32], lhsT=ut_f[:], rhs=fl("lf"), start=True, stop=True),
               r=["ut_f", "lf"], w=["M0"])
            PE(lambda e: e.matmul(Mb[0][:, 32:64], lhsT=ones_f[:], rhs=fl("lf"), start=True, stop=True),
               r=["ones_f", "lf"], w=["M0"])
            V(lambda e: e.tensor_copy(out=fl("bcs"), in_=Mb[0][:, 0:32]), r=["M0"], w=["bcs"])
            V(lambda e: e.tensor_copy(out=fl("abc"), in_=Mb[0][:, 32:64]), r=["M0"], w=["abc"])
            V(lambda e: e.tensor_tensor(out=g["gg"][:], in0=g["li"][:], in1=g["bcs"][:], op=ALU.subtract),
              r=["li", "bcs"], w=["gg"])
            V(lambda e: e.tensor_copy(out=ggpad[:, 0:32], in_=fl("gg")), r=["gg"], w=["ggpad"])
            PE(lambda e: e.transpose(out=Mb[1][:, 0:128], in_=ggpad[:], identity=ident_f[:]),
               r=["ggpad", "ident_f"], w=["M1"])
            V(lambda e: e.tensor_reduce(out=gmax[0:32, 0:1], in_=Mb[1][0:32, 0:128], axis=AX.X, op=ALU.max),
              r=["M1"], w=["gmax"])
            V(lambda e: e.tensor_scalar(out=dg[0:32, 0:32], in0=ident_f[0:32, 0:32], scalar1=gmax[0:32, 0:1],
                                        scalar2=None, op0=ALU.mult), r=["gmax", "ident_f"], w=["dg"])
            PE(lambda e: e.matmul(Mb[1][:, 128:160], lhsT=ones_f[:], rhs=dg[:, 0:32], start=True, stop=True),
               r=["dg", "ones_f"], w=["M1"])
            V(lambda e: e.tensor_copy(out=fl("Gb"), in_=Mb[1][:, 128:160]), r=["M1"], w=["Gb"])
            V(lambda e: e.tensor_tensor(out=g["mloc"][:], in0=g["abc"][:], in1=g["Gb"][:], op=ALU.add),
              r=["abc", "Gb"], w=["mloc"])
            for c in range(8):
                V(lambda e, c=c: e.tensor_tensor(out=g["amp"][:, c, :], in0=g["abc"][:, c, :], in1=m_in[:, c, :], op=ALU.add),
                  r=["abc", "m_in"], w=["amp"])
                V(lambda e, c=c: e.tensor_tensor(out=m_in[:, c + 1, :], in0=g["amp"][:, c, :], in1=g["mloc"][:, c, :],
                                                 op=ALU.max), r=["amp", "mloc"], w=["m_in"])

            def sub_exp(dn, en, a_ap, b_ap, rk, scale=1.0, op=ALU.subtract):
                V(lambda e: e.tensor_tensor(out=g[dn][:], in0=a_ap(), in1=b_ap(), op=op), r=rk, w=[dn])
                A(lambda e: e.activation(out=g[en][:], in_=g[dn][:], func=AF.Exp, scale=scale), r=[dn], w=[en])

            sub_exp("d1", "sprev", lambda: g["amp"][:], lambda: m_in[:, 1:9, :], ["amp", "m_in"])
            sub_exp("d2", "sloc", lambda: g["mloc"][:], lambda: m_in[:, 1:9, :], ["mloc", "m_in"])
            V(lambda e: e.tensor_tensor(out=g["Mc"][:], in0=m_in[:, 0:8, :], in1=g["Gb"][:], op=ALU.max),
              r=["m_in", "Gb"], w=["Mc"])
            sub_exp("d3", "cscale", lambda: m_in[:, 0:8, :], lambda: g["Mc"][:], ["m_in", "Mc"])
            V(lambda e: e.tensor_scalar(out=g["cscale"][:], in0=g["cscale"][:], scalar1=QSCALE, scalar2=None, op0=ALU.mult),
              r=["cscale"], w=["cscale"])
            sub_exp("d4", "ehat", lambda: g["gg"][:], lambda: g["Mc"][:], ["gg", "Mc"])
            sub_exp("d5", "estate", lambda: g["gg"][:], lambda: g["Gb"][:], ["gg", "Gb"])
            sub_exp("d6", "clampv", lambda: g["bcs"][:], lambda: g["Mc"][:], ["bcs", "Mc"], scale=-1.0, op=ALU.add)
            V(lambda e: e.tensor_copy(out=m_in[:, 0, :], in_=m_in[:, 8, :]), r=["m_in"], w=["m_in"])

            if mix_level < 3:
                return
            P01a = Pacc[0][:, 0:260].rearrange("p (h e) -> p h e", h=2)
            P01b = Pacc[0][:, 512:772].rearrange("p (h e) -> p h e", h=2)
            P23a = Pacc[1][:, 0:260].rearrange("p (h e) -> p h e", h=2)
            P23b = Pacc[1][:, 512:772].rearrange("p (h e) -> p h e", h=2)
            PN = [P01a, P01a, P01b, P01b]
            PNK = ["P01a", "P01a", "P01b", "P01b"]
            PC = [P23a, P23a, P23b, P23b]
            PCK = ["P23a", "P23a", "P23b", "P23b"]
            T0v = Tbv[0][:, 0:4, :]
            T1v = Tbv[1][:, 0:4, :]
            M1v4 = Mb[1][:, :].rearrange("p (h n) -> p h n", h=4)
            for c in range(8):
                sl = c % 2
                cs = slice(c * 128, (c + 1) * 128)
                ms_ = mst[:, c, :]
                mk = ("mst", c)
                G(lambda e, c=c: e.tensor_tensor(out=Cbf[:, :, 0:129], in0=Cst[:, :, 0:129],
                                                 in1=bc3(g["cscale"][:, c, :], 129), op=ALU.mult),
                  r=["Cst", "cscale"], w=["Cbf"])
                for hh in range(4):
                    PE(lambda e, hh=hh, cs=cs: e.matmul(M0v4[:, hh, :], lhsT=qkT[:, 4 + hh, cs], rhs=qkT[:, hh, cs],
                                                        start=True, stop=True),
                       r=[("qkT", 4 + hh), ("qkT", hh)] + RW, w=["M0"], inc=(hh == 3))
                V(lambda e, sl=sl: e.tensor_tensor(out=qks[sl], in0=M0v4, in1=maskS[:].unsqueeze(1).to_broadcast([128, 4, 128]),
                                                   op=ALU.mult), r=["M0", "maskS"] + RW, w=[("qks", sl)])
                for hh in range(4):
                    PE(lambda e, hh=hh, cs=cs: e.transpose(out=T0v[:, hh, :], in_=qkT[:, 4 + hh, cs], identity=ident_b[:]),
                       r=[("qkT", 4 + hh), "ident_b"] + RW, w=["T0"], inc=(hh == 3))
                V(lambda e, sl=sl, c=c: e.tensor_tensor(out=ke[sl], in0=T0v, in1=bc3(g["estate"][:, c, :], 128), op=ALU.mult),
                  r=["T0", "estate"] + RW, w=[("ke", sl)])
                G(lambda e, sl=sl, c=c: e.tensor_tensor(out=vp[sl][:, :, 0:129], in0=vext[:, c, :, 0:129],
                                                        in1=bc3(g["ehat"][:, c, :], 129), op=ALU.mult),
                  r=[("vext", c), "vext1", "ehat"] + RW, w=[("vp", sl)])
                for hh in range(4):
                    last = (hh % 2 == 1)
                    PE(lambda e, hh=hh, cs=cs: e.matmul(PN[hh][:, hh % 2, 0:129], lhsT=qkT[:, hh, cs], rhs=Cbf[:, hh, 0:129],
                                                        start=True, stop=False),
                       r=[("qkT", hh), "Cbf"] + RW, w=[PNK[hh]], inc=False)
                    PE(lambda e, hh=hh, sl=sl: e.matmul(PN[hh][:, hh % 2, 0:129], lhsT=qks[sl][:, hh, :], rhs=vp[sl][:, hh, 0:129],
                                                        start=False, stop=True),
                       r=[("qks", sl), ("vp", sl)] + RW, w=[PNK[hh]], inc=last)
                for hh in range(4):
                    last = (hh % 2 == 1)
                    PE(lambda e, hh=hh, sl=sl, c=c: e.matmul(PC[hh][:, hh % 2, 0:129], lhsT=ke[sl][:, hh, :],
                                                             rhs=vext[:, c, hh, 0:129], start=True, stop=True),
                       r=[("ke", sl), ("vext", c), "vext1"] + RW, w=[PCK[hh]], inc=last)
                V(lambda e: e.tensor_copy(out=ms_[:, 12:14], in_=P01a[:, :, 128]), r=["P01a"], w=[mk])
                V(lambda e: e.tensor_copy(out=ms_[:, 14:16], in_=P01b[:, :, 128]), r=["P01b"], w=[mk])
                V(lambda e: e.scalar_tensor_tensor(out=ms_[:, 0:4], in0=ms_[:, 12:16], scalar=-1.0, in1=ms_[:, 12:16],
                                                   op0=ALU.mult, op1=ALU.max), r=[mk], w=[mk])
                V(lambda e, c=c: e.tensor_tensor(out=ms_[:, 0:4], in0=ms_[:, 0:4], in1=g["clampv"][:, c, :], op=ALU.max),
                  r=[mk, "clampv"], w=[mk])
                V(lambda e: e.tensor_scalar(out=ms_[:, 0:4], in0=ms_[:, 0:4], scalar1=1e-30, scalar2=None, op0=ALU.max),
                  r=[mk], w=[mk])
                V(lambda e: e.reciprocal(out=ms_[:, 4:8], in_=ms_[:, 0:4]), r=[mk], w=[mk])
                for hh in range(4):
                    A(lambda e, hh=hh, sl=sl: e.activation(out=xnb[sl][:, hh * 128:(hh + 1) * 128], in_=PN[hh][:, hh % 2, 0:128],
                                                           func=AF.Square, accum_out=ms_[:, 8 + hh: 9 + hh]),
                      r=[PNK[hh]], w=[("xnb", sl), ("mssq", c)])
                V(lambda e: e.tensor_tensor(out=ms_[:, 12:16], in0=ms_[:, 4:8], in1=ms_[:, 4:8], op=ALU.mult), r=[mk], w=[mk])
                V(lambda e: e.tensor_tensor(out=ms_[:, 12:16], in0=ms_[:, 12:16], in1=ms_[:, 8:12], op=ALU.mult),
                  r=[mk, ("mssq", c)], w=[mk])
                V(lambda e: e.tensor_scalar(out=ms_[:, 12:16], in0=ms_[:, 12:16], scalar1=1.0 / 128, scalar2=EPS,
                                            op0=ALU.mult, op1=ALU.add), r=[mk], w=[mk])
                A(lambda e: e.activation(out=ms_[:, 0:4], in_=ms_[:, 12:16], func=AF.Sqrt), r=[mk], w=[mk])
                V(lambda e: e.reciprocal(out=ms_[:, 12:16], in_=ms_[:, 0:4]), r=[mk], w=[mk])
                V(lambda e: e.tensor_tensor(out=ms_[:, 0:4], in0=ms_[:, 12:16], in1=ms_[:, 4:8], op=ALU.mult), r=[mk], w=[mk])
                for hh in range(4):
                    V(lambda e, hh=hh, sl=sl, c=c: e.scalar_tensor_tensor(
                        out=hmb[sl][:, hh, :], in0=PN[hh][:, hh % 2, 0:128], scalar=ms_[:, hh:hh + 1],
                        in1=og[:, c, hh * 128:(hh + 1) * 128], op0=ALU.mult, op1=ALU.mult),
                      r=[PNK[hh], mk, ("og", c)] + RW, w=[("hmb", sl)])
                for hh in range(4):
                    PE(lambda e, hh=hh, sl=sl: e.transpose(out=T1v[:, hh, :], in_=hmb[sl][:, hh, :], identity=ident_b[:]),
                       r=[("hmb", sl), "ident_b"] + RW, w=["T1"], inc=(hh == 3))
                A(lambda e, cs=cs: e.activation(out=xnT[:, 0:4, cs], in_=T1v, func=AF.Copy), r=["T1"], w=[("xnT", c)])
                V(lambda e, c=c: e.tensor_tensor(out=tmpC[:, 0:2, 0:129], in0=P23a[:, :, 0:129],
                                                 in1=bc3(g["sloc"][:, c, 0:2], 129), op=ALU.mult),
                  r=["P23a", "sloc"] + RW, w=["tmpC"])
                V(lambda e, c=c: e.tensor_tensor(out=tmpC[:, 2:4, 0:129], in0=P23b[:, :, 0:129],
                                                 in1=bc3(g["sloc"][:, c, 2:4], 129), op=ALU.mult),
                  r=["P23b", "sloc"] + RW, w=["tmpC"])
                G(lambda e, c=c: e.tensor_tensor(out=Cst[:, :, 0:129], in0=Cst[:, :, 0:129],
                                                 in1=bc3(g["sprev"][:, c, :], 129), op=ALU.mult),
                  r=["Cst", "sprev", "Cbf"], w=["Cst"])
                G(lambda e: e.tensor_tensor(out=Cst[:, :, 0:129], in0=Cst[:, :, 0:129], in1=tmpC[:, :, 0:129], op=ALU.add),
                  r=["Cst", "tmpC"] + RW, w=["Cst"])
                for gg_ in range(4):
                    PE(lambda e, gg_=gg_, c=c: e.matmul(M1v4[:, gg_, :], lhsT=vg[:, c, gg_ * 128:(gg_ + 1) * 128],
                                                        rhs=wsT[:, gg_, :], start=True, stop=True),
                       r=[("vg", c), "wsT"] + RW, w=["M1"], inc=(gg_ == 3))
                V(lambda e, sl=sl: e.tensor_tensor(out=scr[sl][:, 0:512].rearrange("p (g t) -> p g t", g=4), in0=M1v4,
                                                   in1=bsp[:], op=ALU.add), r=["M1", "bsp"], w=[("scr", sl)])
                G(lambda e, sl=sl, cs=cs: e.tensor_tensor(out=xnT[:, 4:8, cs], in0=scr[sl][:, 0:512].rearrange("p (g t) -> p g t", g=4),
                                                          in1=uT[:, :, cs], op=ALU.mult),
                  r=[("scr", sl)] + [("uT", j_) for j_ in range(4)] + RW, w=[("xnT", c)])
            if mix_level < 4:
                return
            def wo_group(tt):
                j = tt % 2
                P = Pacc[j]
                PK = PaccK[j]
                for mc in range(8):
                    PE(lambda e, P=P, mc=mc, tt=tt: e.matmul(P[:, 0:512], lhsT=xnT[:, mc, tt * 128:(tt + 1) * 128],
                                                             rhs=wout[:, mc, 0:512], start=(mc == 0), stop=(mc == 7)),
                       r=[("xnT", tt), "wout", "Wg"], w=[PK[0]], inc=False)
                    PE(lambda e, P=P, mc=mc, tt=tt: e.matmul(P[:, 512:1024], lhsT=xnT[:, mc, tt * 128:(tt + 1) * 128],
                                                             rhs=wout[:, mc, 512:1024], start=(mc == 0), stop=(mc == 7)),
                       r=[("xnT", tt), "wout", "Wg"], w=[PK[1]], inc=(mc == 7))

            tail_loop(wo_group, lambda tt: epilogue(Pacc[tt % 2], PaccK[tt % 2], tt, 1), nxt)

        def ple(hf, pre_gi, has_next):
            load_gpost(1, 7, 1.0)
            if pre_gi is not None:
                for tt in range(NTT):
                    prenorm(tt, pre_gi)
            fence("Rg")
            for tt in range(NTT):
                sl = tt % 2
                row0 = hf * TH + tt * 128
                S.dma("sp", lambda e, sl=sl, row0=row0: e.dma_start(out=pt[sl][:], in_=p[row0:row0 + 128, :]),
                      writes=[("pt", sl)])
                V(lambda e, sl=sl: e.tensor_copy(out=ptb[sl][:], in_=pt[sl][:]), r=[("pt", sl)], w=[("ptb", sl)])
                for k in range(2):
                    PE(lambda e, k=k, sl=sl: e.transpose(out=Tbv[sl][:, k, :], in_=ptb[sl][:, k * 128:(k + 1) * 128],
                                                         identity=ident_b[:]),
                       r=[("ptb", sl), "ident_b"], w=[TbK[sl]], inc=(k == 1))
                A(lambda e, sl=sl, tt=tt: e.activation(out=pTs[:, :, tt * 128:(tt + 1) * 128], in_=Tbv[sl][:, 0:2, :], func=AF.Copy),
                  r=[TbK[sl]], w=[("pTs", tt)])
            tiles = seq[(hf, "pg")]
            for cg in range(4):
                slot = ring_need(tiles[cg])
                for tt in range(NTT):
                    j = tt % 2
                    P, PK = Mb[j], MbK[j]
                    for k in range(8):
                        PE(lambda e, k=k, P=P, tt=tt, slot=slot: e.matmul(
                            P[:, 0:256], lhsT=xnT[:, k, tt * 128:(tt + 1) * 128], rhs=ring[slot][:, k, :],
                            start=(k == 0), stop=(k == 7)), r=[("ring", slot), ("xnT", tt)], w=[PK], inc=(k == 7))
                    A(lambda e, P=P, tt=tt, cg=cg: e.activation(out=gate[:, tt, cg * 256:(cg + 1) * 256], in_=P[:, 0:256],
                                                                func=AF.Sigmoid), r=[PK, "Rg"], w=[("gate", tt)])
            for tt in range(NTT):
                j = tt % 2
                sl = tt % 2
                P, PK = Pacc[j], PaccK[j]
                for pc in range(2):
                    PE(lambda e, P=P, pc=pc, tt=tt: e.matmul(P[:, 0:512], lhsT=pTs[:, pc, tt * 128:(tt + 1) * 128],
                                                             rhs=wple[:, pc, 0:512], start=(pc == 0), stop=(pc == 1)),
                       r=[("pTs", tt), "wple"], w=[PK[0]], inc=False)
                    PE(lambda e, P=P, pc=pc, tt=tt: e.matmul(P[:, 512:1024], lhsT=pTs[:, pc, tt * 128:(tt + 1) * 128],
                                                             rhs=wple[:, pc, 512:1024], start=(pc == 0), stop=(pc == 1)),
                       r=[("pTs", tt), "wple"], w=[PK[1]], inc=(pc == 1))
                s, sk = newstat()
                V(lambda e, P=P, tt=tt, sl=sl: e.tensor_tensor(out=scr[sl][:], in0=gate[:, tt, :], in1=P[:, :], op=ALU.mult),
                  r=PK + [("gate", tt), "Rg"], w=[("scr", sl)])
                A(lambda e, sl=sl, s=s: e.activation(out=junk[:], in_=scr[sl][:], func=AF.Square, accum_out=s[:, 0:1]),
                  r=[("scr", sl)], w=[sk])
                rstd_from_ssq(s, sk, D)
                V(lambda e, sl=sl, s=s: e.scalar_tensor_tensor(out=scr[sl][:], in0=scr[sl][:], scalar=s[:, 3:4], in1=gpost[1][:],
                                                               op0=ALU.mult, op1=ALU.mult),
                  r=[("scr", sl), sk, ("gpost", 1)], w=[("scr", sl)])
                V(lambda e, tt=tt, sl=sl: e.tensor_tensor(out=h[:, tt, :], in0=h[:, tt, :], in1=scr[sl][:], op=ALU.add),
                  r=[("h", tt), ("scr", sl)], w=[("h", tt)])
                store_y(hf, tt)
                ykeys.append(("y", hf, tt))
                if has_next:
                    load_x(hf + 1, tt)
                    if tt >= 2:
                        prenorm_stats(tt - 2, 0)
                    if tt >= 3:
                        prenorm_T(tt - 3, 0)
            if has_next:
                prenorm_stats(6, 0)
                prenorm_T(5, 0)
                prenorm_stats(7, 0)
                prenorm_T(6, 0)
                prenorm_T(7, 0)

        ykeys = []

        def load_x(hf, tt):
            row0 = hf * TH + tt * 128
            S.dma("sp", lambda e: e.dma_start(out=h[:, tt, :], in_=x[row0:row0 + 128, :]), writes=[("h", tt)])

        def store_y(hf, tt):
            row0 = hf * TH + tt * 128
            S.dma("sp", lambda e: e.dma_start(out=y[row0:row0 + 128, :], in_=h[:, tt, :]), reads=[("h", tt)],
                  writes=[("y", hf, tt)])

        for tt in range(NTT):
            load_x(0, tt)
        for hf in range(nhalf):
            has_next = hf + 1 < nhalf
            piped = (stage >= 4 and hf > 0)
            ffn(hf, 1, 0, 1, None if piped else 0, 1 if stage >= 2 else None)
            if stage >= 2:
                mixer(hf, None, 2 if stage >= 3 else None)
            if stage >= 3:
                ffn(hf, 2, 0, 5, None, 3 if stage >= 4 else None)
            if stage >= 4:
                ple(hf, None, has_next)
            else:
                for tt in range(NTT):
                    store_y(hf, tt)
                    ykeys.append(("y", hf, tt))
                    if has_next:
                        load_x(hf + 1, tt)
        S.final_wait("sp", ykeys)
        S.run()
    return nc


_WKEYS = ["ffn1_gu", "ffn1_down", "ffn2_gu", "ffn2_down", "w_in", "conv_w", "conv_b", "b_if", "mh_norm_g",
          "gmlp_ln_g", "gmlp_ln_b", "w_spatial", "b_spatial", "w_out", "w_ple", "w_ple_gate", "norm_g"]


def kernel(**inputs):
    from concourse.bass_utils import run_bass_kernel_spmd
    n = 8
    x = np.asarray(inputs["x"], dtype=np.float32)
    p = np.asarray(inputs["p"], dtype=np.float32)
    shared = {k: np.ascontiguousarray(np.asarray(inputs[k], dtype=np.float32)[0]) for k in _WKEYS}
    shared["ident"] = np.eye(128, dtype=np.float32)
    shared["utm"] = np.triu(np.ones((128, 128), dtype=np.float32))
    in_maps = []
    for b in range(n):
        m = dict(shared)
        m["x"] = np.ascontiguousarray(x[b])
        m["p"] = np.ascontiguousarray(p[0, b])
        in_maps.append(m)
    nc = build(stage=4, nhalf=2)
    res = run_bass_kernel_spmd(nc, in_maps, core_ids=list(range(n)))
    return np.stack([np.asarray(r["y"], dtype=np.float32) for r in res.results], axis=0)
```

```python
import numpy as np
from contextlib import ExitStack
import concourse.bass as bass
import concourse.mybir as mybir

F32 = mybir.dt.float32
BF16 = mybir.dt.bfloat16
AF = mybir.ActivationFunctionType
ALU = mybir.AluOpType
AX = mybir.AxisListType


ENGS = ("pe", "act", "dve", "pool", "sp")


class Sched:
    def __init__(self, nc, es):
        self.nc = nc
        self.es = es
        self.sem = {k: es.enter_context(nc.semaphore("s_" + k)) for k in ENGS}
        self.cnt = {k: 0 for k in ENGS}
        self.seen = {k: {} for k in ENGS}
        self.res = {}
        self.prog = {k: [] for k in ENGS}
        self.dsem = {}
        self.pending_inc = {k: None for k in ENGS}
        self.nwaits = 0
        self.excl = set()

    def _r(self, key):
        r = self.res.get(key)
        if r is None:
            r = {"w": None, "r": []}
            self.res[key] = r
        return r

    def _deps(self, eng, reads, writes):
        deps = {}

        def add(d):
            if d is None:
                return
            k, v = d
            if deps.get(k, 0) < v:
                deps[k] = v

        for key in reads:
            r = self._r(key)
            add(r["w"])
            if key in self.excl:
                for d in r["r"]:
                    if d[0] != eng:
                        add(d)
        for key in writes:
            r = self._r(key)
            add(r["w"])
            for d in r["r"]:
                add(d)
        out = []
        for k, v in deps.items():
            if eng == "pe" and k == "pe":
                continue
            if self.seen[eng].get(k, 0) >= v:
                continue
            self.seen[eng][k] = v
            out.append((k, v))
        return out

    def _semobj(self, k):
        if k in self.sem:
            return self.sem[k]
        return self.dsem[k][0]

    def op(self, eng, fn, reads=(), writes=(), inc=True):
        waits = self._deps(eng, reads, writes)
        self.nwaits += len(waits)
        if inc:
            self.cnt[eng] += 1
            stamp = (eng, self.cnt[eng])
        else:
            stamp = (eng, self.cnt[eng] + 1)
        for key in reads:
            self._r(key)["r"].append(stamp)
        for key in writes:
            r = self._r(key)
            r["w"] = stamp
            r["r"] = []
        sem = self.sem[eng]
        wl = [(self._semobj(k), v) for k, v in waits]

        def emit(e):
            for s, v in wl:
                e.wait_ge(s, v)
            ins = fn(e)
            if inc:
                ins.then_inc(sem, 1)

        self.prog[eng].append(emit)
        if not inc:
            self.pending_inc[eng] = True
        else:
            self.pending_inc[eng] = None

    def dma(self, eng, fn, reads=(), writes=(), n=1):
        waits = self._deps(eng, reads, writes)
        self.nwaits += len(waits)
        dkey = "d_" + str(writes[0])
        if dkey not in self.dsem:
            nm = "dm%d" % len(self.dsem)
            self.dsem[dkey] = [self.es.enter_context(self.nc.semaphore(nm)), 0]
        ent = self.dsem[dkey]
        ent[1] += 16 * n
        stamp = (dkey, ent[1])
        for key in reads:
            self._r(key)["r"].append(stamp)
        for key in writes:
            r = self._r(key)
            r["w"] = stamp
            r["r"] = []
        dsem = ent[0]
        wl = [(self._semobj(k), v) for k, v in waits]

        def emit(e):
            for s, v in wl:
                e.wait_ge(s, v)
            ins = fn(e)
            if not isinstance(ins, (list, tuple)):
                ins = [ins]
            assert len(ins) == n, (len(ins), n)
            for i in ins:
                i.then_inc(dsem, 16)

        self.prog[eng].append(emit)

    def final_wait(self, eng, keys):
        waits = self._deps(eng, keys, ())
        wl = [(self._semobj(k), v) for k, v in waits]

        def emit(e):
            for s, v in wl:
                e.wait_ge(s, v)

        self.prog[eng].append(emit)

    def run(self):
        nc = self.nc
        with nc.Block() as block:
            @block.tensor
            def _(e):
                for f in self.prog["pe"]:
                    f(e)

            @block.scalar
            def _(e):
                for f in self.prog["act"]:
                    f(e)

            @block.vector
            def _(e):
                for f in self.prog["dve"]:
                    f(e)

            @block.gpsimd
            def _(e):
                for f in self.prog["pool"]:
                    f(e)

            @block.sync
            def _(e):
                for f in self.prog["sp"]:
                    f(e)


T = 2048
TH = 1024
NTT = 8
D = 1024
DFF = 2816
NFC = 22
EPS = 1e-6
NRING = 3
QSCALE = 128 ** -0.5


def build(stage=4, nhalf=2, mix_level=4):
    nc = bass.Bass("TRN2", target_bir_lowering=False)
    dt_in = lambda name, shape: nc.dram_tensor(name, shape, F32, kind="ExternalInput").ap()
    x = dt_in("x", [T, D])
    p = dt_in("p", [T, 256])
    ffn_gu = {1: dt_in("ffn1_gu", [D, 2 * DFF]), 2: dt_in("ffn2_gu", [D, 2 * DFF])}
    ffn_dn = {1: dt_in("ffn1_down", [DFF, D]), 2: dt_in("ffn2_down", [DFF, D])}
    w_in = dt_in("w_in", [D, 3080])
    conv_w = dt_in("conv_w", [4, 1024])
    conv_b = dt_in("conv_b", [1024])
    b_if = dt_in("b_if", [8])
    mh_g = dt_in("mh_norm_g", [512])
    ln_g = dt_in("gmlp_ln_g", [512])
    ln_b = dt_in("gmlp_ln_b", [512])
    w_sp = dt_in("w_spatial", [4, 128, 128])
    b_sp = dt_in("b_spatial", [4, 128])
    w_out = dt_in("w_out", [D, D])
    w_ple = dt_in("w_ple", [256, D])
    w_pg = dt_in("w_ple_gate", [D, D])
    norm_g = dt_in("norm_g", [8, D])
    ident = dt_in("ident", [128, 128])
    utm = dt_in("utm", [128, 128])
    y = nc.dram_tensor("y", [T, D], F32, kind="ExternalOutput").ap()

    with ExitStack() as es:
        S = Sched(nc, es)
        sb = lambda name, shape, dt: es.enter_context(nc.sbuf_tensor(name, shape, dt))
        ps = lambda name, shape, dt: es.enter_context(nc.psum_tensor(name, shape, dt))

        def V(fn, r=(), w=()):
            S.op("dve", fn, reads=r, writes=w)

        def A(fn, r=(), w=()):
            S.op("act", fn, reads=r, writes=w)

        def G(fn, r=(), w=()):
            S.op("pool", fn, reads=r, writes=w)

        def PE(fn, r=(), w=(), inc=True):
            S.op("pe", fn, reads=r, writes=w, inc=inc)

        h = sb("h", [128, NTT, D], F32)
        xnT = sb("xnT", [128, 8, TH], BF16)
        R = sb("R", [128, 22528], BF16)
        WD = sb("WD", [128, 22528], BF16)
        ring = [sb("ring%d" % i, [128, 8, 256], BF16) for i in range(NRING)]
        wple = sb("wple", [128, 2, D], BF16)
        pTs = sb("pTs", [128, 2, TH], BF16)
        gpost = [sb("gpost%d" % i, [128, D], F32) for i in range(2)]
        gcol = sb("gcol", [128, 4, 8], F32)
        xnb = [sb("xnb%d" % i, [128, D], BF16) for i in range(2)]
        scr = [sb("scr%d" % i, [128, D], F32) for i in range(2)]
        sg = [sb("sg%d" % i, [128, 512], F32) for i in range(2)]
        stt = sb("stt", [128, 16, 8], F32)
        ident_f = sb("ident_f", [128, 128], F32)
        ident_b = sb("ident_b", [128, 128], BF16)
        ut_f = sb("ut_f", [128, 128], F32)
        ones_f = sb("ones_f", [128, 128], F32)
        maskS = sb("maskS", [128, 128], F32)
        cw = sb("cw", [128, 4, 8], F32)
        cb = sb("cb", [128, 8], F32)
        bif = sb("bif", [128, 8], F32)
        mhg = sb("mhg", [128, 512], F32)
        lng = sb("lng", [128, 512], F32)
        lnb = sb("lnb", [128, 512], F32)
        wsT = sb("wsT", [128, 4, 128], BF16)
        bsp = sb("bsp", [128, 4, 128], F32)
        wif = sb("wif", [128, 8, 8], BF16)
        Cst = sb("Cst", [128, 4, 130], F32)
        Cbf = sb("Cbf", [128, 4, 130], BF16)
        carry = sb("carry", [128, 8, 4], F32)
        pt = [sb("pt%d" % i, [128, 256], F32) for i in range(2)]
        ptb = [sb("ptb%d" % i, [128, 256], BF16) for i in range(2)]
        fz = sb("fz", [128, 4], F32)
        epsc = sb("epsc", [128, 1], F32)
        junk = sb("junk", [128, D], BF16)
        gnames = ["gat8", "li", "fpv", "gab", "ge1", "gl1", "gmn", "lf", "bcs", "abc", "gg", "Gb", "mloc", "amp",
                  "d1", "sprev", "d2", "sloc", "Mc", "d3", "cscale", "d4", "ehat", "d5", "estate", "d6", "clampv"]
        GA = {}
        for nm in gnames:
            GA[nm] = sb("g_" + nm, [128, 8, 8 if nm == "gat8" else 4], F32)
        m_in = sb("m_in", [128, 9, 4], F32)
        gmax = sb("gmax", [128, 1], F32)
        dg = sb("dg", [128, 32], F32)
        ggpad = sb("ggpad", [128, 128], F32)
        mst = sb("mst", [128, 8, 16], F32)

        hT = R[:, :].rearrange("p (f t) -> p f t", f=NFC)
        qkT = R[:, 0:8192].rearrange("p (j t) -> p j t", j=8)
        og = R[:, 8192:12288].rearrange("p (c e) -> p c e", c=8)
        uT = R[:, 12288:16384].rearrange("p (j t) -> p j t", j=4)
        vg = R[:, 16384:20480].rearrange("p (c e) -> p c e", c=8)
        gate = R[:, 0:16384].bitcast(F32).rearrange("p (c e) -> p c e", c=8)
        Wdn = WD[:, :].rearrange("p (f n) -> p f n", f=NFC)
        wout = WD[:, 0:8192].rearrange("p (m n) -> p m n", m=8)
        vext = WD[:, 8192:12352].rearrange("p (c h e) -> p c h e", c=8, h=4)
        stg = [WD[:, 12352 + i * 2056: 12352 + (i + 1) * 2056].bitcast(F32) for i in range(2)]
        o0 = 16464
        ke = [WD[:, o0 + i * 512: o0 + (i + 1) * 512].rearrange("p (h d) -> p h d", h=4) for i in range(2)]
        o0 += 1024
        vp = [WD[:, o0 + i * 520: o0 + (i + 1) * 520].rearrange("p (h e) -> p h e", h=4) for i in range(2)]
        o0 += 1040
        qks = [WD[:, o0 + i * 512: o0 + (i + 1) * 512].rearrange("p (h d) -> p h d", h=4) for i in range(2)]
        o0 += 1024
        hmb = [WD[:, o0 + i * 512: o0 + (i + 1) * 512].rearrange("p (h d) -> p h d", h=4) for i in range(2)]
        o0 += 1024
        tmpC = WD[:, o0: o0 + 1040].bitcast(F32).rearrange("p (h e) -> p h e", h=4)
        o0 += 1040
        assert o0 <= 22528

        Pacc = [ps("P01", [128, 1024], F32), ps("P23", [128, 1024], F32)]
        PaccK = [["P01a", "P01b"], ["P23a", "P23b"]]
        Tb = [ps("T0", [128, 512], F32), ps("T1", [128, 512], F32)]
        TbK = ["T0", "T1"]
        Mb = [ps("M0", [128, 512], F32), ps("M1", [128, 512], F32)]
        MbK = ["M0", "M1"]
        S.excl = {"P01a", "P01b", "P23a", "P23b", "T0", "T1", "M0", "M1"}
        Tbv = [t[:, :].bitcast(BF16).rearrange("p (k n) -> p k n", k=8) for t in Tb]

        st_i = [0]

        def newstat():
            st_i[0] = (st_i[0] + 1) % 16
            return stt[:, st_i[0], :], ("st", st_i[0])

        def ld(eng, out_ap, in_ap, key, slow=False):
            S.dma(eng, lambda e: e.dma_start(out=out_ap, in_=in_ap, allow_slow_non_contiguous=slow), writes=[key])

        ld("sp", ident_f[:], ident, "ident_f")
        for a_ in range(4):
            ld("sp", gcol[:, a_, :], norm_g[2 * a_].rearrange("(k p) -> p k", p=128), "gcol", slow=True)
        for tt_ in range(NTT):
            S.dma("sp", lambda e, tt_=tt_: e.dma_start(out=h[:, tt_, :], in_=x[tt_ * 128:(tt_ + 1) * 128, :]),
                  writes=[("h", tt_)])

        ld("sp", ut_f[:], utm, "ut_f")
        for k_ in range(4):
            ld("sp", cw[:, k_, :], conv_w[k_].rearrange("(j p) -> p j", p=128), "cw", slow=True)
        ld("sp", cb[:], conv_b.rearrange("(j p) -> p j", p=128), "cb", slow=True)
        ld("sp", bif[:], b_if.partition_broadcast(128), "bif")
        ld("sp", mhg[:], mh_g.partition_broadcast(128), "mhg")
        ld("sp", lng[:], ln_g.partition_broadcast(128), "lng")
        ld("sp", lnb[:], ln_b.partition_broadcast(128), "lnb")
        ld("sp", bsp[:], b_sp.rearrange("g t -> (g t)").partition_broadcast(128), "bsp")
        ld("sp", scr[0][:, 0:512].rearrange("p (g s) -> p g s", g=4), w_sp.rearrange("g t s -> t g s"), ("scr", 0))
        S.dma("pool", lambda e: e.dma_start(out=wif[:], in_=w_in[:, 2048:2056].rearrange("(k p) n -> p k n", p=128)),
              writes=["wif"])
        S.dma("pool", lambda e: e.dma_start(out=wple[:], in_=w_ple.rearrange("(k p) n -> p k n", p=128)),
              writes=["wple"])

        V(lambda e: e.tensor_copy(out=ident_b[:], in_=ident_f[:]), r=["ident_f"], w=["ident_b"])
        V(lambda e: e.memset(ones_f[:], 1.0), w=["ones_f"])
        V(lambda e: e.tensor_scalar(out=maskS[:], in0=ut_f[:], scalar1=QSCALE, scalar2=None, op0=ALU.mult),
          r=["ut_f"], w=["maskS"])
        V(lambda e: e.memset(Cst[:], 0.0), w=["Cst"])
        V(lambda e: e.memset(carry[:], 0.0), w=["carry"])
        V(lambda e: e.memset(m_in[:], 0.0), w=["m_in"])
        V(lambda e: e.memset(fz[:], 0.0), w=["fz"])
        V(lambda e: e.memset(epsc[:], EPS), w=["epsc"])
        V(lambda e: e.memset(ggpad[:], 0.0), w=["ggpad"])
        V(lambda e: e.memset(dg[:], 0.0), w=["dg"])
        M0v4 = Mb[0][:, :].rearrange("p (h n) -> p h n", h=4)
        for g_ in range(4):
            PE(lambda e, g_=g_: e.transpose(out=M0v4[:, g_, :], in_=scr[0][:, g_ * 128:(g_ + 1) * 128], identity=ident_f[:]),
               r=[("scr", 0), "ident_f"], w=["M0"], inc=(g_ == 3))
        V(lambda e: e.tensor_tensor(out=wsT[:], in0=M0v4, in1=ut_f[:].unsqueeze(1).to_broadcast([128, 4, 128]), op=ALU.mult),
          r=["M0", "ut_f"], w=["wsT"])

        ring_specs = []

        def ring_add(srcs):
            ring_specs.append(srcs)
            return len(ring_specs) - 1

        ring_issued = [0]

        def ring_need(i):
            while ring_issued[0] < len(ring_specs) and ring_issued[0] <= i + NRING - 1:
                j = ring_issued[0]
                slot = j % NRING
                srcs = ring_specs[j]

                def fn(e, srcs=srcs, slot=slot):
                    out = []
                    for (c0, c1, src) in srcs:
                        out.append(e.dma_start(out=ring[slot][:, :, c0:c1], in_=src.rearrange("(k p) n -> p k n", p=128)))
                    return out

                S.dma("pool", fn, writes=[("ring", slot)], n=len(srcs))
                ring_issued[0] += 1
            return i % NRING

        seq = {}
        for hf in range(nhalf):
            for f in (1, 2):
                if f == 1 or stage >= 3:
                    seq[(hf, "gu", f)] = [ring_add([(0, 128, ffn_gu[f][:, fc * 128:(fc + 1) * 128]),
                                                     (128, 256, ffn_gu[f][:, DFF + fc * 128: DFF + (fc + 1) * 128])])
                                          for fc in range(NFC)]
                if f == 1 and stage >= 2:
                    cols = [1024, 1280, 1536, 1792, 2568, 2824, 0, 256, 512, 768, 2056, 2312]
                    seq[(hf, "win")] = [ring_add([(0, 256, w_in[:, c:c + 256])]) for c in cols]
            if stage >= 4:
                seq[(hf, "pg")] = [ring_add([(0, 256, w_pg[:, c * 256:(c + 1) * 256])]) for c in range(4)]

        fence_n = [0]

        def fence(key):
            fence_n[0] += 1
            V(lambda e: e.memset(fz[:, 0:1], 0.0), w=[key, "fzs"])

        def load_gpost(slot, idx, scale):
            S.dma("sp", lambda e: e.dma_start(out=gpost[slot][:], in_=norm_g[idx].partition_broadcast(128)),
                  writes=[("gpost", slot)])
            if scale != 1.0:
                V(lambda e: e.tensor_scalar(out=gpost[slot][:], in0=gpost[slot][:], scalar1=scale, scalar2=None, op0=ALU.mult),
                  r=[("gpost", slot)], w=[("gpost", slot)])

        def rstd_from_ssq(s, sk, n, col_in=0):
            A(lambda e: e.activation(out=s[:, 2:3], in_=s[:, col_in:col_in + 1], func=AF.Sqrt, scale=1.0 / n, bias=epsc[:, 0:1]),
              r=[sk, "epsc"], w=[sk])
            V(lambda e: e.reciprocal(out=s[:, 3:4], in_=s[:, 2:3]), r=[sk], w=[sk])

        def prenorm_stats(tt, gi):
            s, sk = newstat()
            sl = tt % 2
            A(lambda e: e.activation(out=junk[:], in_=h[:, tt, :], func=AF.Square, accum_out=s[:, 0:1]),
              r=[("h", tt)], w=[sk])
            rstd_from_ssq(s, sk, D)
            A(lambda e: e.activation(out=xnb[sl][:], in_=h[:, tt, :], func=AF.Copy, scale=s[:, 3:4]),
              r=[("h", tt), sk], w=[("xnb", sl)])

        def prenorm_T(tt, gi):
            sl = tt % 2
            for k in range(8):
                PE(lambda e, k=k: e.transpose(out=Tbv[sl][:, k, :], in_=xnb[sl][:, k * 128:(k + 1) * 128], identity=ident_b[:]),
                   r=[("xnb", sl), "ident_b"], w=[TbK[sl]], inc=(k == 7))
            V(lambda e: e.tensor_tensor(out=xnT[:, :, tt * 128:(tt + 1) * 128], in0=Tbv[sl],
                                        in1=gcol[:, gi, :].unsqueeze(2).to_broadcast([128, 8, 128]), op=ALU.mult),
              r=[TbK[sl], "gcol"], w=[("xnT", tt)])

        def prenorm(tt, gi):
            prenorm_stats(tt, gi)
            prenorm_T(tt, gi)

        def tail_loop(group_fn, epi_fn, nxt):
            for tt in range(NTT):
                group_fn(tt)
                if nxt is not None and tt >= 2:
                    prenorm_T(tt - 2, nxt)
                epi_fn(tt)
                if nxt is not None and tt >= 1:
                    prenorm_stats(tt - 1, nxt)
            if nxt is not None:
                prenorm_stats(NTT - 1, nxt)
                prenorm_T(NTT - 2, nxt)
                prenorm_T(NTT - 1, nxt)

        def epilogue(P, PK, tt, gslot):
            s, sk = newstat()
            sl = tt % 2
            A(lambda e: e.activation(out=junk[:], in_=P[:, :], func=AF.Square, accum_out=s[:, 0:1]),
              r=PK, w=[sk])
            rstd_from_ssq(s, sk, D)
            V(lambda e: e.scalar_tensor_tensor(out=scr[sl][:], in0=P[:, :], scalar=s[:, 3:4], in1=gpost[gslot][:],
                                               op0=ALU.mult, op1=ALU.mult),
              r=PK + [sk, ("gpost", gslot)], w=[("scr", sl)])
            V(lambda e: e.tensor_tensor(out=h[:, tt, :], in0=h[:, tt, :], in1=scr[sl][:], op=ALU.add),
              r=[("h", tt), ("scr", sl)], w=[("h", tt)])

        def ffn(hf, f, gslot, gidx, pre_gi, nxt):
            load_gpost(gslot, gidx, 0.5)
            if pre_gi is not None:
                for tt in range(NTT):
                    prenorm(tt, pre_gi)
            fence("Rg")
            fence("Wg")
            tiles = seq[(hf, "gu", f)]
            for fc in range(NFC):
                slot = ring_need(tiles[fc])
                S.dma("pool", lambda e, fc=fc: e.dma_start(out=Wdn[:, fc, :], in_=ffn_dn[f][fc * 128:(fc + 1) * 128, :]),
                      reads=["Wg"], writes=[("Wdn", fc)])
                for tb in range(2):
                    j = (fc * 2 + tb) % 2
                    P, PK = Pacc[j], PaccK[j]
                    xk = [("xnT", tb * 4 + i) for i in range(4)]
                    for k in range(8):
                        PE(lambda e, k=k, P=P, slot=slot, tb=tb: e.matmul(
                            P[:, 0:512], lhsT=ring[slot][:, k, 0:128], rhs=xnT[:, k, tb * 512:(tb + 1) * 512],
                            start=(k == 0), stop=(k == 7)), r=[("ring", slot)] + xk, w=[PK[0]], inc=False)
                    for k in range(8):
                        PE(lambda e, k=k, P=P, slot=slot, tb=tb: e.matmul(
                            P[:, 512:1024], lhsT=ring[slot][:, k, 128:256], rhs=xnT[:, k, tb * 512:(tb + 1) * 512],
                            start=(k == 0), stop=(k == 7)), r=[("ring", slot)] + xk, w=[PK[1]], inc=(k == 7))
                    A(lambda e, P=P, j=j: e.activation(out=sg[j][:], in_=P[:, 0:512], func=AF.Silu),
                      r=[PK[0]], w=[("sg", j)])
                    V(lambda e, P=P, j=j, fc=fc, tb=tb: e.tensor_tensor(
                        out=hT[:, fc, tb * 512:(tb + 1) * 512], in0=sg[j][:], in1=P[:, 512:1024], op=ALU.mult),
                      r=[("sg", j), PK[1], "Rg"], w=[("hT", tb)])
            def dn_group(tt):
                j = tt % 2
                P = Pacc[j]
                PK = PaccK[j]
                for fc in range(NFC):
                    PE(lambda e, P=P, fc=fc, tt=tt: e.matmul(
                        P[:, 0:512], lhsT=hT[:, fc, tt * 128:(tt + 1) * 128], rhs=Wdn[:, fc, 0:512],
                        start=(fc == 0), stop=(fc == NFC - 1)),
                       r=[("hT", tt // 4), ("Wdn", fc), "Rg", "Wg"], w=[PK[0]], inc=False)
                    PE(lambda e, P=P, fc=fc, tt=tt: e.matmul(
                        P[:, 512:1024], lhsT=hT[:, fc, tt * 128:(tt + 1) * 128], rhs=Wdn[:, fc, 512:1024],
                        start=(fc == 0), stop=(fc == NFC - 1)),
                       r=[("hT", tt // 4), ("Wdn", fc), "Rg", "Wg"], w=[PK[1]], inc=(fc == NFC - 1))

            tail_loop(dn_group, lambda tt: epilogue(Pacc[tt % 2], PaccK[tt % 2], tt, gslot), nxt)

        def bc3(ap2, n):
            return ap2.unsqueeze(2).to_broadcast([128, ap2.shape[1], n])

        def mixer(hf, pre_gi, nxt):
            load_gpost(1, 3, 1.0)
            if pre_gi is not None:
                for tt in range(NTT):
                    prenorm(tt, pre_gi)
            fence("Rg")
            fence("Wg")
            S.dma("pool", lambda e: e.dma_start(out=wout, in_=w_out.rearrange("(k p) n -> p k n", p=128)),
                  reads=["Wg"], writes=["wout"])
            V(lambda e: e.memset(vext[:, :, :, 128:130], 1.0), r=["Wg"], w=["vext1"])
            tiles = seq[(hf, "win")]
            RW = ["Rg", "Wg"]

            def tokproj(t0, evac):
                s0 = ring_need(tiles[t0])
                s1 = tiles[t0 + 1] % NRING
                for tt in range(NTT):
                    j = tt % 2
                    P, PK = Pacc[j], PaccK[j]
                    for half, sl_ in ((0, s0), (1, s1)):
                        for k in range(8):
                            PE(lambda e, k=k, P=P, sl_=sl_, half=half, tt=tt: e.matmul(
                                P[:, half * 256:(half + 1) * 256], lhsT=xnT[:, k, tt * 128:(tt + 1) * 128],
                                rhs=ring[sl_][:, k, :], start=(k == 0), stop=(k == 7)),
                               r=[("ring", sl_), ("xnT", tt)], w=[PK[0]], inc=(k == 7 and half == 1))
                    evac(tt, P, PK)

            def evac_v(tt, P, PK):
                A(lambda e: e.activation(out=vext[:, tt, :, 0:128], in_=P[:, 0:512].rearrange("p (h e) -> p h e", h=4),
                                         func=AF.Copy), r=[PK[0]] + RW, w=[("vext", tt)])
                for k in range(8):
                    PE(lambda e, k=k: e.matmul(Mb[0][:, 0:8], lhsT=xnT[:, k, tt * 128:(tt + 1) * 128], rhs=wif[:, k, :],
                                               start=(k == 0), stop=(k == 7)),
                       r=["wif", ("xnT", tt)], w=["M0"], inc=(k == 7))
                V(lambda e: e.tensor_copy(out=GA["gat8"][:, tt, :], in_=Mb[0][:, 0:8]), r=["M0"], w=["gat8"])

            def evac_o(tt, P, PK):
                sl = tt % 2
                A(lambda e: e.activation(out=sg[sl][:], in_=P[:, 0:512], func=AF.Sigmoid), r=[PK[0]], w=[("sg", sl)])
                G(lambda e: e.tensor_tensor(out=og[:, tt, :], in0=sg[sl][:], in1=mhg[:], op=ALU.mult),
                  r=[("sg", sl), "mhg"] + RW, w=[("og", tt)])

            def evac_vg(tt, P, PK):
                sl = tt % 2
                s, sk = newstat()
                A(lambda e: e.activation(out=sg[sl][:], in_=P[:, 0:512], func=AF.Gelu), r=[PK[0]], w=[("sg", sl)])
                V(lambda e: e.bn_stats(out=s[:, 0:6], in_=sg[sl][:]), r=[("sg", sl)], w=[sk])
                s2, sk2 = newstat()
                V(lambda e: e.bn_aggr(out=s2[:, 4:6], in_=s[:, 0:6]), r=[sk], w=[sk2])
                A(lambda e: e.activation(out=s2[:, 2:3], in_=s2[:, 5:6], func=AF.Sqrt, bias=epsc[:, 0:1]), r=[sk2, "epsc"], w=[sk2])
                V(lambda e: e.reciprocal(out=s2[:, 3:4], in_=s2[:, 2:3]), r=[sk2], w=[sk2])
                V(lambda e: e.tensor_scalar(out=scr[sl][:, 0:512], in0=sg[sl][:], scalar1=s2[:, 4:5], scalar2=s2[:, 3:4],
                                            op0=ALU.subtract, op1=ALU.mult), r=[("sg", sl), sk2], w=[("scr", sl)])
                G(lambda e: e.tensor_tensor(out=scr[sl][:, 0:512], in0=scr[sl][:, 0:512], in1=lng[:], op=ALU.mult),
                  r=[("scr", sl), "lng"], w=[("scr", sl)])
                G(lambda e: e.tensor_tensor(out=vg[:, tt, :], in0=scr[sl][:, 0:512], in1=lnb[:], op=ALU.add),
                  r=[("scr", sl), "lnb"] + RW, w=[("vg", tt)])

            tokproj(0, evac_v)
            tokproj(2, evac_o)
            tokproj(4, evac_vg)

            def featproj(ti, evac):
                slot = ring_need(tiles[ti])
                for cc in range(2):
                    for tb in range(2):
                        j = (cc * 2 + tb) % 2
                        P, PK = Mb[j], MbK[j]
                        xk = [("xnT", tb * 4 + i) for i in range(4)]
                        for k in range(8):
                            PE(lambda e, k=k, P=P, cc=cc, tb=tb: e.matmul(
                                P[:, :], lhsT=ring[slot][:, k, cc * 128:(cc + 1) * 128],
                                rhs=xnT[:, k, tb * 512:(tb + 1) * 512], start=(k == 0), stop=(k == 7)),
                               r=[("ring", slot)] + xk, w=[PK], inc=(k == 7))
                        evac(cc, tb, P, PK)

            def qk_evac(jbase):
                def ev(cc, tb, P, PK):
                    jj = jbase + cc
                    sl = jj % 2
                    if tb == 0:
                        V(lambda e: e.tensor_copy(out=stg[sl][:, 0:3], in_=carry[:, jj, 0:3]),
                          r=["carry"] + RW, w=[("stg", sl)])
                    A(lambda e: e.activation(out=stg[sl][:, 3 + tb * 512: 3 + (tb + 1) * 512], in_=P[:, :], func=AF.Copy),
                      r=[PK] + RW, w=[("stg", sl)])
                    if tb == 1:
                        acc = scr[sl]
                        V(lambda e: e.tensor_scalar(out=acc[:], in0=stg[sl][:, 3:1027], scalar1=cw[:, 3, jj:jj + 1],
                                                    scalar2=cb[:, jj:jj + 1], op0=ALU.mult, op1=ALU.add),
                          r=[("stg", sl), "cw", "cb"] + RW, w=[("scr", sl)])
                        for kk in (2, 1, 0):
                            V(lambda e, kk=kk: e.scalar_tensor_tensor(out=acc[:], in0=stg[sl][:, kk:kk + 1024],
                                                                      scalar=cw[:, kk, jj:jj + 1], in1=acc[:],
                                                                      op0=ALU.mult, op1=ALU.add),
                              r=[("stg", sl), "cw", ("scr", sl)] + RW, w=[("scr", sl)])
                        A(lambda e: e.activation(out=qkT[:, jj, :], in_=acc[:], func=AF.Silu),
                          r=[("scr", sl)] + RW, w=[("qkT", jj)])
                        V(lambda e: e.tensor_copy(out=carry[:, jj, 0:3], in_=stg[sl][:, 1024:1027]),
                          r=[("stg", sl)] + RW, w=["carry"])
                return ev

            def u_evac(jbase):
                def ev(cc, tb, P, PK):
                    jj = jbase + cc
                    A(lambda e: e.activation(out=uT[:, jj, tb * 512:(tb + 1) * 512], in_=P[:, :], func=AF.Gelu),
                      r=[PK] + RW, w=[("uT", jj)])
                return ev

            featproj(6, qk_evac(0))
            featproj(7, qk_evac(2))
            featproj(8, qk_evac(4))
            featproj(9, qk_evac(6))
            featproj(10, u_evac(0))
            featproj(11, u_evac(2))

            if mix_level < 2:
                return
            g = GA
            fl = lambda nm: g[nm][:, :, :].rearrange("p c h -> p (c h)")
            V(lambda e: e.tensor_tensor(out=g["li"][:], in0=g["gat8"][:, :, 0:4],
                                        in1=bif[:, 0:4].unsqueeze(1).to_broadcast([128, 8, 4]), op=ALU.add),
              r=["gat8", "bif"], w=["li"])
            V(lambda e: e.tensor_tensor(out=g["fpv"][:], in0=g["gat8"][:, :, 4:8],
                                        in1=bif[:, 4:8].unsqueeze(1).to_broadcast([128, 8, 4]), op=ALU.add),
              r=["gat8", "bif"], w=["fpv"])
            V(lambda e: e.scalar_tensor_tensor(out=g["gab"][:], in0=g["fpv"][:], scalar=-1.0, in1=g["fpv"][:],
                                               op0=ALU.mult, op1=ALU.min), r=["fpv"], w=["gab"])
            A(lambda e: e.activation(out=g["ge1"][:], in_=g["gab"][:], func=AF.Exp), r=["gab"], w=["ge1"])
            A(lambda e: e.activation(out=g["gl1"][:], in_=g["ge1"][:], func=AF.Ln, bias=1.0), r=["ge1"], w=["gl1"])
            V(lambda e: e.tensor_single_scalar(out=g["gmn"][:], in_=g["fpv"][:], scalar=0.0, op=ALU.min),
              r=["fpv"], w=["gmn"])
            V(lambda e: e.tensor_tensor(out=g["lf"][:], in0=g["gmn"][:], in1=g["gl1"][:], op=ALU.subtract),
              r=["gmn", "gl1"], w=["lf"])
            PE(lambda e: e.matmul(Mb[0][:, 0:32], lhsT=ut_f[:], rhs=fl("lf"), start=True, stop=True),
               r=["ut_f", "lf"], w=["M0"])
            PE(lambda e: e.matmul(Mb[0][:, 32:64], lhsT=ones_f[:], rhs=fl("lf"), start=True, stop=True),
               r=["ones_f", "lf"], w=["M0"])
            V(lambda e: e.tensor_copy(out=fl("bcs"), in_=Mb[0][:, 0:32]), r=["M0"], w=["bcs"])
            V(lambda e: e.tensor_copy(out=fl("abc"), in_=Mb[0][:, 32:64]), r=["M0"], w=["abc"])
            V(lambda e: e.tensor_tensor(out=g["gg"][:], in0=g["li"][:], in1=g["bcs"][:], op=ALU.subtract),
              r=["li", "bcs"], w=["gg"])
            V(lambda e: e.tensor_copy(out=ggpad[:, 0:32], in_=fl("gg")), r=["gg"], w=["ggpad"])
            PE(lambda e: e.transpose(out=Mb[1][:, 0:128], in_=ggpad[:], identity=ident_f[:]),
               r=["ggpad", "ident_f"], w=["M1"])
            V(lambda e: e.tensor_reduce(out=gmax[0:32, 0:1], in_=Mb[1][0:32, 0:128], axis=AX.X, op=ALU.max),
              r=["M1"], w=["gmax"])
            V(lambda e: e.tensor_scalar(out=dg[0:32, 0:32], in0=ident_f[0:32, 0:32], scalar1=gmax[0:32, 0:1],
                                        scalar2=None, op0=ALU.mult), r=["gmax", "ident_f"], w=["dg"])
            PE(lambda e: e.matmul(Mb[1][:, 128:160], lhsT=ones_f[:], rhs=dg[:, 0:32], start=True, stop=True),
               r=["dg", "ones_f"], w=["M1"])
            V(lambda e: e.tensor_copy(out=fl("Gb"), in_=Mb[1][:, 128:160]), r=["M1"], w=["Gb"])
            V(lambda e: e.tensor_tensor(out=g["mloc"][:], in0=g["abc"][:], in1=g["Gb"][:], op=ALU.add),
              r=["abc", "Gb"], w=["mloc"])
            for c in range(8):
                V(lambda e, c=c: e.tensor_tensor(out=g["amp"][:, c, :], in0=g["abc"][:, c, :], in1=m_in[:, c, :], op=ALU.add),
                  r=["abc", "m_in"], w=["amp"])
                V(lambda e, c=c: e.tensor_tensor(out=m_in[:, c + 1, :], in0=g["amp"][:, c, :], in1=g["mloc"][:, c, :],
                                                 op=ALU.max), r=["amp", "mloc"], w=["m_in"])

            def sub_exp(dn, en, a_ap, b_ap, rk, scale=1.0, op=ALU.subtract):
                V(lambda e: e.tensor_tensor(out=g[dn][:], in0=a_ap(), in1=b_ap(), op=op), r=rk, w=[dn])
                A(lambda e: e.activation(out=g[en][:], in_=g[dn][:], func=AF.Exp, scale=scale), r=[dn], w=[en])

            sub_exp("d1", "sprev", lambda: g["amp"][:], lambda: m_in[:, 1:9, :], ["amp", "m_in"])
            sub_exp("d2", "sloc", lambda: g["mloc"][:], lambda: m_in[:, 1:9, :], ["mloc", "m_in"])
            V(lambda e: e.tensor_tensor(out=g["Mc"][:], in0=m_in[:, 0:8, :], in1=g["Gb"][:], op=ALU.max),
              r=["m_in", "Gb"], w=["Mc"])
            sub_exp("d3", "cscale", lambda: m_in[:, 0:8, :], lambda: g["Mc"][:], ["m_in", "Mc"])
            V(lambda e: e.tensor_scalar(out=g["cscale"][:], in0=g["cscale"][:], scalar1=QSCALE, scalar2=None, op0=ALU.mult),
              r=["cscale"], w=["cscale"])
            sub_exp("d4", "ehat", lambda: g["gg"][:], lambda: g["Mc"][:], ["gg", "Mc"])
            sub_exp("d5", "estate", lambda: g["gg"][:], lambda: g["Gb"][:], ["gg", "Gb"])
            sub_exp("d6", "clampv", lambda: g["bcs"][:], lambda: g["Mc"][:], ["bcs", "Mc"], scale=-1.0, op=ALU.add)
            V(lambda e: e.tensor_copy(out=m_in[:, 0, :], in_=m_in[:, 8, :]), r=["m_in"], w=["m_in"])

            if mix_level < 3:
                return
            P01a = Pacc[0][:, 0:260].rearrange("p (h e) -> p h e", h=2)
            P01b = Pacc[0][:, 512:772].rearrange("p (h e) -> p h e", h=2)
            P23a = Pacc[1][:, 0:260].rearrange("p (h e) -> p h e", h=2)
            P23b = Pacc[1][:, 512:772].rearrange("p (h e) -> p h e", h=2)
            PN = [P01a, P01a, P01b, P01b]
            PNK = ["P01a", "P01a", "P01b", "P01b"]
            PC = [P23a, P23a, P23b, P23b]
            PCK = ["P23a", "P23a", "P23b", "P23b"]
            T0v = Tbv[0][:, 0:4, :]
            T1v = Tbv[1][:, 0:4, :]
            M1v4 = Mb[1][:, :].rearrange("p (h n) -> p h n", h=4)
            for c in range(8):
                sl = c % 2
                cs = slice(c * 128, (c + 1) * 128)
                ms_ = mst[:, c, :]
                mk = ("mst", c)
                G(lambda e, c=c: e.tensor_tensor(out=Cbf[:, :, 0:129], in0=Cst[:, :, 0:129],
                                                 in1=bc3(g["cscale"][:, c, :], 129), op=ALU.mult),
                  r=["Cst", "cscale"], w=["Cbf"])
                for hh in range(4):
                    PE(lambda e, hh=hh, cs=cs: e.matmul(M0v4[:, hh, :], lhsT=qkT[:, 4 + hh, cs], rhs=qkT[:, hh, cs],
                                                        start=True, stop=True),
                       r=[("qkT", 4 + hh), ("qkT", hh)] + RW, w=["M0"], inc=(hh == 3))
                V(lambda e, sl=sl: e.tensor_tensor(out=qks[sl], in0=M0v4, in1=maskS[:].unsqueeze(1).to_broadcast([128, 4, 128]),
                                                   op=ALU.mult), r=["M0", "maskS"] + RW, w=[("qks", sl)])
                for hh in range(4):
                    PE(lambda e, hh=hh, cs=cs: e.transpose(out=T0v[:, hh, :], in_=qkT[:, 4 + hh, cs], identity=ident_b[:]),
                       r=[("qkT", 4 + hh), "ident_b"] + RW, w=["T0"], inc=(hh == 3))
                V(lambda e, sl=sl, c=c: e.tensor_tensor(out=ke[sl], in0=T0v, in1=bc3(g["estate"][:, c, :], 128), op=ALU.mult),
                  r=["T0", "estate"] + RW, w=[("ke", sl)])
                G(lambda e, sl=sl, c=c: e.tensor_tensor(out=vp[sl][:, :, 0:129], in0=vext[:, c, :, 0:129],
                                                        in1=bc3(g["ehat"][:, c, :], 129), op=ALU.mult),
                  r=[("vext", c), "vext1", "ehat"] + RW, w=[("vp", sl)])
                for hh in range(4):
                    last = (hh % 2 == 1)
                    PE(lambda e, hh=hh, cs=cs: e.matmul(PN[hh][:, hh % 2, 0:129], lhsT=qkT[:, hh, cs], rhs=Cbf[:, hh, 0:129],
                                                        start=True, stop=False),
                       r=[("qkT", hh), "Cbf"] + RW, w=[PNK[hh]], inc=False)
                    PE(lambda e, hh=hh, sl=sl: e.matmul(PN[hh][:, hh % 2, 0:129], lhsT=qks[sl][:, hh, :], rhs=vp[sl][:, hh, 0:129],
                                                        start=False, stop=True),
                       r=[("qks", sl), ("vp", sl)] + RW, w=[PNK[hh]], inc=last)
                for hh in range(4):
                    last = (hh % 2 == 1)
                    PE(lambda e, hh=hh, sl=sl, c=c: e.matmul(PC[hh][:, hh % 2, 0:129], lhsT=ke[sl][:, hh, :],
                                                             rhs=vext[:, c, hh, 0:129], start=True, stop=True),
                       r=[("ke", sl), ("vext", c), "vext1"] + RW, w=[PCK[hh]], inc=last)
                V(lambda e: e.tensor_copy(out=ms_[:, 12:14], in_=P01a[:, :, 128]), r=["P01a"], w=[mk])
                V(lambda e: e.tensor_copy(out=ms_[:, 14:16], in_=P01b[:, :, 128]), r=["P01b"], w=[mk])
                V(lambda e: e.scalar_tensor_tensor(out=ms_[:, 0:4], in0=ms_[:, 12:16], scalar=-1.0, in1=ms_[:, 12:16],
                                                   op0=ALU.mult, op1=ALU.max), r=[mk], w=[mk])
                V(lambda e, c=c: e.tensor_tensor(out=ms_[:, 0:4], in0=ms_[:, 0:4], in1=g["clampv"][:, c, :], op=ALU.max),
                  r=[mk, "clampv"], w=[mk])
                V(lambda e: e.tensor_scalar(out=ms_[:, 0:4], in0=ms_[:, 0:4], scalar1=1e-30, scalar2=None, op0=ALU.max),
                  r=[mk], w=[mk])
                V(lambda e: e.reciprocal(out=ms_[:, 4:8], in_=ms_[:, 0:4]), r=[mk], w=[mk])
                for hh in range(4):
                    A(lambda e, hh=hh, sl=sl: e.activation(out=xnb[sl][:, hh * 128:(hh + 1) * 128], in_=PN[hh][:, hh % 2, 0:128],
                                                           func=AF.Square, accum_out=ms_[:, 8 + hh: 9 + hh]),
                      r=[PNK[hh]], w=[("xnb", sl), ("mssq", c)])
                V(lambda e: e.tensor_tensor(out=ms_[:, 12:16], in0=ms_[:, 4:8], in1=ms_[:, 4:8], op=ALU.mult), r=[mk], w=[mk])
                V(lambda e: e.tensor_tensor(out=ms_[:, 12:16], in0=ms_[:, 12:16], in1=ms_[:, 8:12], op=ALU.mult),
                  r=[mk, ("mssq", c)], w=[mk])
                A(lambda e: e.activation(out=ms_[:, 0:4], in_=ms_[:, 12:16], func=AF.Sqrt, scale=1.0 / 128, bias=epsc[:, 0:1]),
                  r=[mk, "epsc"], w=[mk])
                V(lambda e: e.reciprocal(out=ms_[:, 12:16], in_=ms_[:, 0:4]), r=[mk], w=[mk])
                V(lambda e: e.tensor_tensor(out=ms_[:, 0:4], in0=ms_[:, 12:16], in1=ms_[:, 4:8], op=ALU.mult), r=[mk], w=[mk])
                for hh in range(4):
                    V(lambda e, hh=hh, sl=sl, c=c: e.scalar_tensor_tensor(
                        out=hmb[sl][:, hh, :], in0=PN[hh][:, hh % 2, 0:128], scalar=ms_[:, hh:hh + 1],
                        in1=og[:, c, hh * 128:(hh + 1) * 128], op0=ALU.mult, op1=ALU.mult),
                      r=[PNK[hh], mk, ("og", c)] + RW, w=[("hmb", sl)])
                for hh in range(4):
                    PE(lambda e, hh=hh, sl=sl: e.transpose(out=T1v[:, hh, :], in_=hmb[sl][:, hh, :], identity=ident_b[:]),
                       r=[("hmb", sl), "ident_b"] + RW, w=["T1"], inc=(hh == 3))
                A(lambda e, cs=cs: e.activation(out=xnT[:, 0:4, cs], in_=T1v, func=AF.Copy), r=["T1"], w=[("xnT", c)])
                V(lambda e, c=c: e.tensor_tensor(out=tmpC[:, 0:2, 0:129], in0=P23a[:, :, 0:129],
                                                 in1=bc3(g["sloc"][:, c, 0:2], 129), op=ALU.mult),
                  r=["P23a", "sloc"] + RW, w=["tmpC"])
                V(lambda e, c=c: e.tensor_tensor(out=tmpC[:, 2:4, 0:129], in0=P23b[:, :, 0:129],
                                                 in1=bc3(g["sloc"][:, c, 2:4], 129), op=ALU.mult),
                  r=["P23b", "sloc"] + RW, w=["tmpC"])
                G(lambda e, c=c: e.tensor_tensor(out=Cst[:, :, 0:129], in0=Cst[:, :, 0:129],
                                                 in1=bc3(g["sprev"][:, c, :], 129), op=ALU.mult),
                  r=["Cst", "sprev", "Cbf"], w=["Cst"])
                G(lambda e: e.tensor_tensor(out=Cst[:, :, 0:129], in0=Cst[:, :, 0:129], in1=tmpC[:, :, 0:129], op=ALU.add),
                  r=["Cst", "tmpC"] + RW, w=["Cst"])
                for gg_ in range(4):
                    PE(lambda e, gg_=gg_, c=c: e.matmul(M1v4[:, gg_, :], lhsT=vg[:, c, gg_ * 128:(gg_ + 1) * 128],
                                                        rhs=wsT[:, gg_, :], start=True, stop=True),
                       r=[("vg", c), "wsT"] + RW, w=["M1"], inc=(gg_ == 3))
                V(lambda e, sl=sl: e.tensor_tensor(out=scr[sl][:, 0:512].rearrange("p (g t) -> p g t", g=4), in0=M1v4,
                                                   in1=bsp[:], op=ALU.add), r=["M1", "bsp"], w=[("scr", sl)])
                G(lambda e, sl=sl, cs=cs: e.tensor_tensor(out=xnT[:, 4:8, cs], in0=scr[sl][:, 0:512].rearrange("p (g t) -> p g t", g=4),
                                                          in1=uT[:, :, cs], op=ALU.mult),
                  r=[("scr", sl)] + [("uT", j_) for j_ in range(4)] + RW, w=[("xnT", c)])
            if mix_level < 4:
                return
            def wo_group(tt):
                j = tt % 2
                P = Pacc[j]
                PK = PaccK[j]
                for mc in range(8):
                    PE(lambda e, P=P, mc=mc, tt=tt: e.matmul(P[:, 0:512], lhsT=xnT[:, mc, tt * 128:(tt + 1) * 128],
                                                             rhs=wout[:, mc, 0:512], start=(mc == 0), stop=(mc == 7)),
                       r=[("xnT", tt), "wout", "Wg"], w=[PK[0]], inc=False)
                    PE(lambda e, P=P, mc=mc, tt=tt: e.matmul(P[:, 512:1024], lhsT=xnT[:, mc, tt * 128:(tt + 1) * 128],
                                                             rhs=wout[:, mc, 512:1024], start=(mc == 0), stop=(mc == 7)),
                       r=[("xnT", tt), "wout", "Wg"], w=[PK[1]], inc=(mc == 7))

            tail_loop(wo_group, lambda tt: epilogue(Pacc[tt % 2], PaccK[tt % 2], tt, 1), nxt)

        def ple(hf, pre_gi, has_next):
            load_gpost(1, 7, 1.0)
            if pre_gi is not None:
                for tt in range(NTT):
                    prenorm(tt, pre_gi)
            fence("Rg")
            for tt in range(NTT):
                sl = tt % 2
                row0 = hf * TH + tt * 128
                S.dma("sp", lambda e, sl=sl, row0=row0: e.dma_start(out=pt[sl][:], in_=p[row0:row0 + 128, :]),
                      writes=[("pt", sl)])
                V(lambda e, sl=sl: e.tensor_copy(out=ptb[sl][:], in_=pt[sl][:]), r=[("pt", sl)], w=[("ptb", sl)])
                for k in range(2):
                    PE(lambda e, k=k, sl=sl: e.transpose(out=Tbv[sl][:, k, :], in_=ptb[sl][:, k * 128:(k + 1) * 128],
                                                         identity=ident_b[:]),
                       r=[("ptb", sl), "ident_b"], w=[TbK[sl]], inc=(k == 1))
                A(lambda e, sl=sl, tt=tt: e.activation(out=pTs[:, :, tt * 128:(tt + 1) * 128], in_=Tbv[sl][:, 0:2, :], func=AF.Copy),
                  r=[TbK[sl]], w=[("pTs", tt)])
            tiles = seq[(hf, "pg")]
            for cg in range(4):
                slot = ring_need(tiles[cg])
                for tt in range(NTT):
                    j = tt % 2
                    P, PK = Mb[j], MbK[j]
                    for k in range(8):
                        PE(lambda e, k=k, P=P, tt=tt, slot=slot: e.matmul(
                            P[:, 0:256], lhsT=xnT[:, k, tt * 128:(tt + 1) * 128], rhs=ring[slot][:, k, :],
                            start=(k == 0), stop=(k == 7)), r=[("ring", slot), ("xnT", tt)], w=[PK], inc=(k == 7))
                    A(lambda e, P=P, tt=tt, cg=cg: e.activation(out=gate[:, tt, cg * 256:(cg + 1) * 256], in_=P[:, 0:256],
                                                                func=AF.Sigmoid), r=[PK, "Rg"], w=[("gate", tt)])
            for tt in range(NTT):
                j = tt % 2
                sl = tt % 2
                P, PK = Pacc[j], PaccK[j]
                for pc in range(2):
                    PE(lambda e, P=P, pc=pc, tt=tt: e.matmul(P[:, 0:512], lhsT=pTs[:, pc, tt * 128:(tt + 1) * 128],
                                                             rhs=wple[:, pc, 0:512], start=(pc == 0), stop=(pc == 1)),
                       r=[("pTs", tt), "wple"], w=[PK[0]], inc=False)
                    PE(lambda e, P=P, pc=pc, tt=tt: e.matmul(P[:, 512:1024], lhsT=pTs[:, pc, tt * 128:(tt + 1) * 128],
                                                             rhs=wple[:, pc, 512:1024], start=(pc == 0), stop=(pc == 1)),
                       r=[("pTs", tt), "wple"], w=[PK[1]], inc=(pc == 1))
                s, sk = newstat()
                V(lambda e, P=P, tt=tt, sl=sl: e.tensor_tensor(out=scr[sl][:], in0=gate[:, tt, :], in1=P[:, :], op=ALU.mult),
                  r=PK + [("gate", tt), "Rg"], w=[("scr", sl)])
                A(lambda e, sl=sl, s=s: e.activation(out=junk[:], in_=scr[sl][:], func=AF.Square, accum_out=s[:, 0:1]),
                  r=[("scr", sl)], w=[sk])
                rstd_from_ssq(s, sk, D)
                V(lambda e, sl=sl, s=s: e.scalar_tensor_tensor(out=scr[sl][:], in0=scr[sl][:], scalar=s[:, 3:4], in1=gpost[1][:],
                                                               op0=ALU.mult, op1=ALU.mult),
                  r=[("scr", sl), sk, ("gpost", 1)], w=[("scr", sl)])
                V(lambda e, tt=tt, sl=sl: e.tensor_tensor(out=h[:, tt, :], in0=h[:, tt, :], in1=scr[sl][:], op=ALU.add),
                  r=[("h", tt), ("scr", sl)], w=[("h", tt)])
                store_y(hf, tt)
                ykeys.append(("y", hf, tt))
                if has_next:
                    load_x(hf + 1, tt)
                    if tt >= 2:
                        prenorm_stats(tt - 2, 0)
                    if tt >= 3:
                        prenorm_T(tt - 3, 0)
            if has_next:
                prenorm_stats(6, 0)
                prenorm_T(5, 0)
                prenorm_stats(7, 0)
                prenorm_T(6, 0)
                prenorm_T(7, 0)

        ykeys = []

        def load_x(hf, tt):
            row0 = hf * TH + tt * 128
            S.dma("sp", lambda e: e.dma_start(out=h[:, tt, :], in_=x[row0:row0 + 128, :]), writes=[("h", tt)])

        def store_y(hf, tt):
            row0 = hf * TH + tt * 128
            S.dma("sp", lambda e: e.dma_start(out=y[row0:row0 + 128, :], in_=h[:, tt, :]), reads=[("h", tt)],
                  writes=[("y", hf, tt)])

        for hf in range(nhalf):
            has_next = hf + 1 < nhalf
            piped = (stage >= 4 and hf > 0)
            ffn(hf, 1, 0, 1, None if piped else 0, 1 if stage >= 2 else None)
            if stage >= 2:
                mixer(hf, None, 2 if stage >= 3 else None)
            if stage >= 3:
                ffn(hf, 2, 0, 5, None, 3 if stage >= 4 else None)
            if stage >= 4:
                ple(hf, None, has_next)
            else:
                for tt in range(NTT):
                    store_y(hf, tt)
                    ykeys.append(("y", hf, tt))
                    if has_next:
                        load_x(hf + 1, tt)
        S.final_wait("sp", ykeys)
        S.run()
    return nc


_WKEYS = ["ffn1_gu", "ffn1_down", "ffn2_gu", "ffn2_down", "w_in", "conv_w", "conv_b", "b_if", "mh_norm_g",
          "gmlp_ln_g", "gmlp_ln_b", "w_spatial", "b_spatial", "w_out", "w_ple", "w_ple_gate", "norm_g"]


def kernel(**inputs):
    from concourse.bass_utils import run_bass_kernel_spmd
    n = 8
    x = np.asarray(inputs["x"], dtype=np.float32)
    p = np.asarray(inputs["p"], dtype=np.float32)
    shared = {k: np.ascontiguousarray(np.asarray(inputs[k], dtype=np.float32)[0]) for k in _WKEYS}
    shared["ident"] = np.eye(128, dtype=np.float32)
    shared["utm"] = np.triu(np.ones((128, 128), dtype=np.float32))
    in_maps = []
    for b in range(n):
        m = dict(shared)
        m["x"] = np.ascontiguousarray(x[b])
        m["p"] = np.ascontiguousarray(p[0, b])
        in_maps.append(m)
    nc = build(stage=4, nhalf=2)
    res = run_bass_kernel_spmd(nc, in_maps, core_ids=list(range(n)))
    return np.stack([np.asarray(r["y"], dtype=np.float32) for r in res.results], axis=0)
```
